# Optimizing a Trainium2 kernel written in Bass

```python
import math
import jax, jax.numpy as jnp
from jax import lax
import numpy as np

D_MODEL = 2048
BATCH = 2
SEQ = 16384
DEPTH = 1

PLE_DIM = 256
D_MIX = D_MODEL
MLSTM_HEADS = 4
MLSTM_HEAD_DIM = (D_MIX // 2) // MLSTM_HEADS
MLSTM_WIDTH = MLSTM_HEADS * MLSTM_HEAD_DIM
MLSTM_CHUNK = 64
FOX_HEADS = 8
FOX_HEAD_DIM = (D_MIX - MLSTM_WIDTH) // FOX_HEADS
FOX_WIDTH = FOX_HEADS * FOX_HEAD_DIM
FOX_BLOCK = 128
D_FF = ((8 * D_MODEL // 3 + 255) // 256) * 256
CONV_WIDTH = 3
EPS = 1e-6
IN_SIZES = (MLSTM_WIDTH, MLSTM_WIDTH, MLSTM_WIDTH, MLSTM_WIDTH, MLSTM_HEADS, MLSTM_HEADS,
            FOX_WIDTH, FOX_WIDTH, FOX_WIDTH, FOX_HEADS)
IN_TOTAL = sum(IN_SIZES)

kernel_name = "hymba_style_mlstm_fox_convffn_ple"


def rms_norm(x, g):
    xf = x.astype(jnp.float32)
    y = xf * lax.rsqrt(jnp.mean(xf * xf, axis=-1, keepdims=True) + EPS)
    return (y * g.astype(jnp.float32)).astype(x.dtype)


def mlstm_chunkwise(q, k, v, ig, lf):
    B, H, S, d = q.shape
    L = MLSTM_CHUNK
    nc = S // L
    k = k * (d ** -0.5)

    def to_chunks(t):
        return jnp.moveaxis(t.reshape((B, H, nc, L) + t.shape[3:]), 2, 0)

    qc, kc, vc = to_chunks(q), to_chunks(k), to_chunks(v)
    igc = to_chunks(ig)
    bc = jnp.cumsum(to_chunks(lf), axis=-1)
    causal = jnp.tril(jnp.ones((L, L), dtype=bool))

    def step(carry, inp):
        C, n, m = carry
        qi, ki, vi, ii, bi = inp
        inter = bi + m[..., None]
        D = jnp.where(causal, bi[..., :, None] - bi[..., None, :] + ii[..., None, :], -jnp.inf)
        m_comb = jnp.maximum(inter, jnp.max(D, axis=-1))
        w_intra = jnp.exp(D - m_comb[..., None])
        w_inter = jnp.exp(inter - m_comb)
        s = jnp.einsum('bhtd,bhsd->bhts', qi, ki) * w_intra
        num = jnp.einsum('bhts,bhsv->bhtv', s, vi) + w_inter[..., None] * jnp.einsum('bhtk,bhkv->bhtv', qi, C)
        den = jnp.sum(s, axis=-1) + w_inter * jnp.einsum('bhtk,bhk->bht', qi, n)
        h = num / jnp.maximum(jnp.abs(den), jnp.exp(-m_comb))[..., None]
        bL = bi[..., -1]
        g = bL[..., None] - bi + ii
        m_new = jnp.maximum(bL + m, jnp.max(g, axis=-1))
        wk = jnp.exp(g - m_new[..., None])
        decay = jnp.exp(bL + m - m_new)
        C = decay[..., None, None] * C + jnp.einsum('bhsk,bhsv->bhkv', ki * wk[..., None], vi)
        n = decay[..., None] * n + jnp.einsum('bhs,bhsk->bhk', wk, ki)
        return (C, n, m_new), h

    init = (jnp.zeros((B, H, d, d), jnp.float32), jnp.zeros((B, H, d), jnp.float32),
            jnp.zeros((B, H), jnp.float32))
    _, hs = lax.scan(step, init, (qc, kc, vc, igc, bc))
    return jnp.moveaxis(hs, 0, 2).reshape(B, H, S, d)


def forgetting_attention(q, k, v, logf):
    B, S, H, d = q.shape
    nb = S // FOX_BLOCK
    F = jnp.cumsum(logf, axis=1).transpose(0, 2, 1)
    kt = k.transpose(0, 2, 1, 3)
    vt = v.transpose(0, 2, 1, 3)
    qb = q.reshape(B, nb, FOX_BLOCK, H, d).transpose(1, 0, 3, 2, 4)
    Fq = F.reshape(B, H, nb, FOX_BLOCK).transpose(2, 0, 1, 3)
    kpos = jnp.arange(S)
    scale = d ** -0.5

    def one_block(args):
        qi, Fqi, bi = args
        qpos = bi * FOX_BLOCK + jnp.arange(FOX_BLOCK)
        s = jnp.einsum('bhqd,bhkd->bhqk', qi, kt, preferred_element_type=jnp.float32) * scale
        s = s + Fqi[..., None] - F[:, :, None, :]
        s = jnp.where(kpos[None, :] <= qpos[:, None], s, -jnp.inf)
        pr = jax.nn.softmax(s, axis=-1)
        return jnp.einsum('bhqk,bhkd->bhqd', pr.astype(vt.dtype), vt)

    out = lax.map(one_block, (qb, Fq, jnp.arange(nb)))
    return out.transpose(1, 0, 3, 2, 4).reshape(B, S, H * d)


def causal_dwconv(a, w, b):
    y = lax.conv_general_dilated(a, w[:, None, :].astype(a.dtype), window_strides=(1,),
                                 padding=[(CONV_WIDTH - 1, 0)],
                                 dimension_numbers=('NWC', 'WIO', 'NWC'),
                                 feature_group_count=a.shape[-1])
    return y + b.astype(a.dtype)


def setup_inputs(seed: int = 0) -> dict:
    key = jax.random.key(seed)
    ks = jax.random.split(key, 20)
    nrm = jax.random.normal
    f32 = jnp.float32
    x = nrm(ks[0], (BATCH, SEQ, D_MODEL), f32)
    p = nrm(ks[1], (DEPTH, BATCH, SEQ, PLE_DIM), f32)
    mix_norm_g = 1.0 + 0.02 * nrm(ks[2], (DEPTH, D_MODEL), f32)
    w_in = nrm(ks[3], (DEPTH, D_MODEL, IN_TOTAL), f32) * D_MODEL ** -0.5
    m_i_bias = 0.1 * nrm(ks[4], (DEPTH, MLSTM_HEADS), f32)
    m_f_bias = jnp.linspace(3.0, 6.0, MLSTM_HEADS, dtype=f32)[None, :] + 0.01 * nrm(ks[5], (DEPTH, MLSTM_HEADS), f32)
    mlstm_gate_bias = jnp.concatenate([m_i_bias, m_f_bias], axis=-1)
    mlstm_out_g = 1.0 + 0.02 * nrm(ks[6], (DEPTH, MLSTM_WIDTH), f32)
    fox_q_g = 1.0 + 0.02 * nrm(ks[7], (DEPTH, FOX_HEAD_DIM), f32)
    fox_k_g = 1.0 + 0.02 * nrm(ks[8], (DEPTH, FOX_HEAD_DIM), f32)
    fox_f_bias = jnp.linspace(1.0, 5.0, FOX_HEADS, dtype=f32)[None, :] + 0.01 * nrm(ks[9], (DEPTH, FOX_HEADS), f32)
    w_out = nrm(ks[10], (DEPTH, D_MIX, D_MODEL), f32) * D_MIX ** -0.5
    ffn_norm_g = 1.0 + 0.02 * nrm(ks[11], (DEPTH, D_MODEL), f32)
    w_up = nrm(ks[12], (DEPTH, D_MODEL, 2 * D_FF), f32) * D_MODEL ** -0.5
    conv_w = nrm(ks[13], (DEPTH, CONV_WIDTH, D_FF), f32) * CONV_WIDTH ** -0.5
    conv_b = 0.02 * nrm(ks[14], (DEPTH, D_FF), f32)
    w_down = nrm(ks[15], (DEPTH, D_FF, D_MODEL), f32) * D_FF ** -0.5
    ple_norm_g = 1.0 + 0.02 * nrm(ks[16], (DEPTH, D_MODEL), f32)
    w_ple_gate = nrm(ks[17], (DEPTH, D_MODEL, D_MODEL), f32) * D_MODEL ** -0.5
    w_ple_proj = nrm(ks[18], (DEPTH, PLE_DIM, D_MODEL), f32) * PLE_DIM ** -0.5
    return {"x": x, "p": p, "mix_norm_g": mix_norm_g, "w_in": w_in,
            "mlstm_gate_bias": mlstm_gate_bias, "mlstm_out_g": mlstm_out_g,
            "fox_q_g": fox_q_g, "fox_k_g": fox_k_g, "fox_f_bias": fox_f_bias,
            "w_out": w_out, "ffn_norm_g": ffn_norm_g, "w_up": w_up, "conv_w": conv_w,
            "conv_b": conv_b, "w_down": w_down, "ple_norm_g": ple_norm_g,
            "w_ple_gate": w_ple_gate, "w_ple_proj": w_ple_proj}


def reference(x, p, mix_norm_g, w_in, mlstm_gate_bias, mlstm_out_g, fox_q_g, fox_k_g, fox_f_bias,
              w_out, ffn_norm_g, w_up, conv_w, conv_b, w_down, ple_norm_g, w_ple_gate, w_ple_proj):
    B, S, _ = x.shape
    f32 = jnp.float32
    split_points = np.cumsum(IN_SIZES)[:-1].tolist()
    for i in range(DEPTH):
        h = rms_norm(x, mix_norm_g[i])
        z = h @ w_in[i]
        mq, mk, mv, mo, mi, mf, fq, fk, fv, ff = jnp.split(z, split_points, axis=-1)

        def heads_a(t):
            return t.reshape(B, S, MLSTM_HEADS, MLSTM_HEAD_DIM).transpose(0, 2, 1, 3).astype(f32)
        ig = (mi + mlstm_gate_bias[i, :MLSTM_HEADS]).astype(f32).transpose(0, 2, 1)
        lf = jax.nn.log_sigmoid((mf + mlstm_gate_bias[i, MLSTM_HEADS:]).astype(f32)).transpose(0, 2, 1)
        ha = mlstm_chunkwise(heads_a(mq), heads_a(mk), heads_a(mv), ig, lf)
        ha = rms_norm(ha.transpose(0, 2, 1, 3), mlstm_out_g[i].reshape(MLSTM_HEADS, MLSTM_HEAD_DIM))
        ha = (ha.reshape(B, S, MLSTM_WIDTH) * jax.nn.sigmoid(mo.astype(f32))).astype(x.dtype)

        qb = rms_norm(fq.reshape(B, S, FOX_HEADS, FOX_HEAD_DIM), fox_q_g[i])
        kb = rms_norm(fk.reshape(B, S, FOX_HEADS, FOX_HEAD_DIM), fox_k_g[i])
        vb = fv.reshape(B, S, FOX_HEADS, FOX_HEAD_DIM)
        logf = jax.nn.log_sigmoid((ff + fox_f_bias[i]).astype(f32))
        hb = forgetting_attention(qb, kb, vb, logf).astype(x.dtype)

        x = x + jnp.concatenate([ha, hb], axis=-1) @ w_out[i]

        h = rms_norm(x, ffn_norm_g[i])
        u = h @ w_up[i]
        a, g = jnp.split(u, 2, axis=-1)
        a = causal_dwconv(a, conv_w[i], conv_b[i])
        x = x + (jax.nn.gelu(a, approximate=False) * g) @ w_down[i]

        gate = jax.nn.sigmoid(rms_norm(x, ple_norm_g[i]) @ w_ple_gate[i])
        x = x + gate * (p[i].astype(x.dtype) @ w_ple_proj[i])
    return x
```

```python
import numpy as np
import ml_dtypes
from contextlib import ExitStack
import concourse.bass as bass
import concourse.mybir as mybir
from concourse.bass_utils import run_bass_kernel_spmd

F32 = mybir.dt.float32
BF16 = mybir.dt.bfloat16
AF = mybir.ActivationFunctionType
ALU = mybir.AluOpType
NPBF = ml_dtypes.bfloat16

D = 2048
NFC = 16
DFF = 5632
NHC = 44
PLE = 256
EPS = 1e-6
DEBUG = {}


class Buf:
    __slots__ = ("lw", "rd")

    def __init__(self):
        self.lw = None
        self.rd = {}


class Chan:
    def __init__(self, sem):
        self.sem = sem
        self.n = 0


class Op:
    __slots__ = ("eng", "fn", "deps", "signal", "count", "chan")


class Prog:
    CE = ("pe", "act", "dve", "pool")

    def __init__(self, nc, stack):
        self.nc = nc
        self.stack = stack
        self.ops = {e: [] for e in self.CE + ("sp",)}
        self.sem = {e: stack.enter_context(nc.semaphore("sem_" + e)) for e in self.CE}
        self.nchan = 0

    def chan(self):
        self.nchan += 1
        return Chan(self.stack.enter_context(self.nc.semaphore("ch%d" % self.nchan)))

    def op(self, eng, fn, reads=(), writes=(), chan=None):
        o = Op()
        o.eng = eng
        o.fn = fn
        o.signal = False
        o.count = 0
        o.chan = chan
        deps = {}
        raw = set()
        for b in reads:
            if b.lw is not None:
                deps[id(b.lw)] = b.lw
                raw.add(id(b.lw))
        for b in writes:
            if b.lw is not None:
                deps[id(b.lw)] = b.lw
            for r in b.rd.values():
                deps[id(r)] = r
        o.deps = []
        for d in deps.values():
            if d is o:
                continue
            if d.chan is not None or d.eng != eng or chan is not None or (id(d) in raw and eng != "pe"):
                o.deps.append(d)
                if d.chan is None:
                    d.signal = True
        if chan is not None:
            chan.n += 16
            o.count = chan.n
        key = chan if chan is not None else eng
        for b in reads:
            b.rd[id(key)] = o
        for b in writes:
            b.lw = o
            b.rd = {}
        self.ops[eng].append(o)
        return o

    def finalize(self):
        for e in self.CE:
            c = 0
            for o in self.ops[e]:
                if o.signal:
                    c += 1
                    o.count = c

    def emit(self, eng, e):
        seen = {}
        for o in self.ops[eng]:
            for d in o.deps:
                s = d.chan.sem if d.chan is not None else self.sem[d.eng]
                k = id(s)
                if seen.get(k, 0) >= d.count:
                    continue
                seen[k] = d.count
                e.wait_ge(s, d.count)
            ins = o.fn(e)
            if ins is None:
                continue
            if o.chan is not None:
                ins.then_inc(o.chan.sem, 16)
            elif o.signal:
                ins.then_inc(self.sem[eng], 1)

    def run(self):
        self.finalize()
        with self.nc.Block() as block:
            @block.tensor
            def _(e):
                self.emit("pe", e)

            @block.scalar
            def _(e):
                self.emit("act", e)

            @block.vector
            def _(e):
                self.emit("dve", e)

            @block.gpsimd
            def _(e):
                self.emit("pool", e)

            @block.sync
            def _(e):
                self.emit("sp", e)


def sb(nc, stack, name, shape, dt):
    return stack.enter_context(nc.sbuf_tensor(name, list(shape), dt))


def ps_bank(nc, stack, name, dt=F32, n=512):
    return stack.enter_context(nc.psum_tensor(name, [128, n], dt))


def build_phase2(NTOK, T=512):
    nc = bass.Bass("TRN2", target_bir_lowering=False)
    NC2 = NTOK + 2
    ntile = NTOK // T
    din = lambda n, s, dt=F32: nc.dram_tensor(n, list(s), dt, kind="ExternalInput").ap()
    mixT = din("mixT", [128, NFC, NC2], BF16)
    xT = din("xT", [128, NFC, NC2])
    pT = din("pT", [128, 2, NTOK])
    wout = din("wout", [16, 128, NFC * 128])
    wup = din("wup", [NHC, 128, NFC * 256])
    wdn = din("wdn", [16, 128, NHC * 128])
    wpg = din("wpg", [16, 128, NFC * 128])
    wpp = din("wpp", [128, 2 * D])
    vecs = din("vecs", [128, 16 + 16 + 44 * 4 + 1])
    outT = nc.dram_tensor("outT", [128, NFC, NTOK], F32, kind="ExternalOutput").ap()
    woutb = nc.dram_tensor("woutb", [16, 128, NFC * 128], BF16).ap()
    wupb = nc.dram_tensor("wupb", [NHC, 128, NFC * 256], BF16).ap()
    wdnb = nc.dram_tensor("wdnb", [16, 128, NHC * 128], BF16).ap()
    wpgb = nc.dram_tensor("wpgb", [16, 128, NFC * 128], BF16).ap()
    wppb = nc.dram_tensor("wppb", [128, 2 * D], BF16).ap()

    with ExitStack() as st:
        P = Prog(nc, st)
        x_t = sb(nc, st, "x_t", [128, NFC, T], F32)
        mix_t = sb(nc, st, "mix_t", [128, NFC, T], BF16)
        sq_t = sb(nc, st, "sq_t", [128, NFC, T], BF16)
        xn_t = sb(nc, st, "xn_t", [128, NFC, T], BF16)
        h_t = sb(nc, st, "h_t", [128, 22, T], BF16)
        p_t = sb(nc, st, "p_t", [128, 2, T], F32)
        pb_t = sb(nc, st, "pb_t", [128, 2, T], BF16)
        rs_t = sb(nc, st, "rs_t", [128, T], F32)
        NA = 2
        abuf = [sb(nc, st, "abuf%d" % i, [128, T + 2], F32) for i in range(NA)]
        ybuf = [sb(nc, st, "ybuf%d" % i, [128, T], F32) for i in range(NA)]
        gbuf = [sb(nc, st, "gbuf%d" % i, [128, T], F32) for i in range(NA)]
        sgbuf = [sb(nc, st, "sgbuf%d" % i, [128, T], F32) for i in range(NA)]
        carry = sb(nc, st, "carry", [128, NHC, 2], F32)
        vec = sb(nc, st, "vec", [128, 16 + 16 + 44 * 4 + 1], F32)
        ones = sb(nc, st, "ones", [128, 128], BF16)
        wppsb = sb(nc, st, "wppsb", [128, 2 * D], BF16)
        NWA, NWU, NWD = 2, 3, 2
        wA = [sb(nc, st, "wA%d" % i, [128, NFC * 128], BF16) for i in range(NWA)]
        wU = [sb(nc, st, "wU%d" % i, [128, NFC * 256], BF16) for i in range(NWU)]
        wD = [sb(nc, st, "wD%d" % i, [128, 22 * 128], BF16) for i in range(NWD)]
        NPS = 8
        ps = [ps_bank(nc, st, "ps%d" % i) for i in range(NPS)]

        B = lambda: Buf()
        b_x, b_mix, b_sq, b_xn, b_h, b_p, b_pb, b_rs, b_carry, b_vec, b_ones, b_wpp = [B() for _ in range(12)]
        b_ab = [B() for _ in range(NA)]
        b_y = [B() for _ in range(NA)]
        b_g = [B() for _ in range(NA)]
        b_sg = [B() for _ in range(NA)]
        b_wA = [B() for _ in range(NWA)]
        b_wU = [B() for _ in range(NWU)]
        b_wD = [B() for _ in range(NWD)]
        b_ps = [B() for _ in range(NPS)]
        b_out = B()
        c_x, c_mix, c_p, c_vec, c_wpp, c_out = [P.chan() for _ in range(6)]
        c_wA = [P.chan() for _ in range(NWA)]
        c_wU = [P.chan() for _ in range(NWU)]
        c_wD = [P.chan() for _ in range(NWD)]
        cnt = {"ps": 0, "wA": 0, "wU": 0, "wD": 0, "ab": 0}

        def nxt(k, n):
            i = cnt[k] % n
            cnt[k] += 1
            return i

        P.op("sp", lambda e: e.dma_start(out=vec[:], in_=vecs), writes=[b_vec], chan=c_vec)
        P.op("pool", lambda e: e.memset(ones[:], 1.0), writes=[b_ones])
        P.op("pool", lambda e: e.memset(carry[:], 0.0), writes=[b_carry])
        b_wcast = {}

        def cast(name, dst, src, pieces):
            n = dst.shape[0]
            step = (n + pieces - 1) // pieces
            for i in range(0, n, step):
                bb = B()
                ch = P.chan()
                P.op("pool", lambda e, i=i: e.dma_start(out=dst[i:i + step], in_=src[i:i + step],
                                                         max_dma_last_dim=4096), writes=[bb], chan=ch)
                for j in range(i, min(n, i + step)):
                    b_wcast[(name, j)] = bb

        cast("wout", woutb, wout, 2)
        cast("wup", wupb, wup, 11)
        cast("wdn", wdnb, wdn, 4)
        cast("wpg", wpgb, wpg, 2)
        bb = B()
        P.op("pool", lambda e: e.dma_start(out=wppb, in_=wpp, max_dma_last_dim=4096), writes=[bb], chan=P.chan())
        P.op("sp", lambda e: e.dma_start(out=wppsb[:], in_=wppb), reads=[bb], writes=[b_wpp], chan=c_wpp)

        GF, GP, CW, CB, FLAG = 0, 16, 32, 32 + 132, 32 + 176

        def mm_group(pst, bps, T_, n, lhs, rhs, rbufs):
            def fn(e):
                ins = None
                for k in range(n):
                    ins = e.matmul(pst[:, :T_], lhs(k), rhs(k), start=(k == 0), stop=(k == n - 1))
                return ins
            P.op("pe", fn, reads=rbufs, writes=[bps])

        def rmsnorm(T_, goff):
            P.op("act", lambda e: e.activation(out=sq_t[:, :, :T_], in_=x_t[:, :, :T_], func=AF.Square),
                 reads=[b_x], writes=[b_sq])
            i = nxt("ps", NPS)
            mm_group(ps[i], b_ps[i], T_, NFC, lambda k: ones[:], lambda k: sq_t[:, k, :T_], [b_ones, b_sq])
            P.op("act", lambda e: e.activation(out=rs_t[:, :T_], in_=ps[i][:, :T_], func=AF.Sqrt,
                                               bias=EPS, scale=1.0 / D), reads=[b_ps[i]], writes=[b_rs])
            P.op("dve", lambda e: e.reciprocal(out=rs_t[:, :T_], in_=rs_t[:, :T_]), reads=[b_rs], writes=[b_rs])

            def fn(e):
                ins = None
                for oc in range(NFC):
                    ins = e.scalar_tensor_tensor(out=xn_t[:, oc, :T_], in0=x_t[:, oc, :T_],
                                                 scalar=vec[:, goff + oc:goff + oc + 1], in1=rs_t[:, :T_],
                                                 op0=ALU.mult, op1=ALU.mult)
                return ins
            P.op("dve", fn, reads=[b_x, b_rs, b_vec], writes=[b_xn])

        def tile(c0, T_, halo, first=False, nxtile=None):
            lo = 2 if first else 0
            if first:
                P.op("sp", lambda e: e.dma_start(out=mix_t[:, :, :T_], in_=mixT[:, :, c0:c0 + T_]),
                     writes=[b_mix], chan=c_mix)
            P.op("sp", lambda e: e.dma_start(out=x_t[:, :, :T_], in_=xT[:, :, c0:c0 + T_]),
                 writes=[b_x], chan=c_x)
            stage = DEBUG.get("stage", 9)
            if stage == 0:
                if not halo:
                    P.op("sp", lambda e: e.dma_start(out=outT[:, :, c0 - 2:c0 - 2 + T_], in_=x_t[:, :, :T_]),
                         reads=[b_x], writes=[b_out], chan=c_out)
                return
            for oc in range(16):
                s = nxt("wA", NWA)
                P.op("sp", lambda e, s=s, oc=oc: e.dma_start(out=wA[s][:], in_=woutb[oc]),
                     reads=[b_wcast[("wout", oc)]], writes=[b_wA[s]], chan=c_wA[s])
                i = nxt("ps", NPS)
                mm_group(ps[i], b_ps[i], T_, NFC, lambda k, s=s: wA[s][:, k * 128:(k + 1) * 128],
                         lambda k: mix_t[:, k, :T_], [b_wA[s], b_mix])
                P.op("dve", lambda e, i=i, oc=oc: e.tensor_tensor(out=x_t[:, oc, :T_], in0=x_t[:, oc, :T_],
                                                                   in1=ps[i][:, :T_], op=ALU.add),
                     reads=[b_ps[i], b_x], writes=[b_x])
            if nxtile is not None:
                P.op("sp", lambda e: e.dma_start(out=mix_t[:, :, :nxtile[1]], in_=mixT[:, :, nxtile[0]:nxtile[0] + nxtile[1]]),
                     writes=[b_mix], chan=c_mix)
            if stage == 1:
                if not halo:
                    P.op("sp", lambda e: e.dma_start(out=outT[:, :, c0 - 2:c0 - 2 + T_], in_=x_t[:, :, :T_]),
                         reads=[b_x], writes=[b_out], chan=c_out)
                return
            rmsnorm(T_, GF)
            if stage == 2:
                if not halo:
                    P.op("act", lambda e: e.activation(out=x_t[:, :, :T_], in_=xn_t[:, :, :T_], func=AF.Identity),
                         reads=[b_xn], writes=[b_x])
                    P.op("sp", lambda e: e.dma_start(out=outT[:, :, c0 - 2:c0 - 2 + T_], in_=x_t[:, :, :T_]),
                         reads=[b_x], writes=[b_out], chan=c_out)
                return
            for half in range(2):
                for hl in range(22):
                    hc = half * 22 + hl
                    s = nxt("wU", NWU)
                    P.op("sp", lambda e, s=s, hc=hc: e.dma_start(out=wU[s][:], in_=wupb[hc]),
                         reads=[b_wcast[("wup", hc)]], writes=[b_wU[s]], chan=c_wU[s])
                    ia = nxt("ps", NPS)
                    mm_group(ps[ia], b_ps[ia], T_, NFC, lambda k, s=s: wU[s][:, k * 256:k * 256 + 128],
                             lambda k: xn_t[:, k, :T_], [b_wU[s], b_xn])
                    a = nxt("ab", NA)
                    P.op("dve", lambda e, a=a, hc=hc: e.tensor_copy(out=abuf[a][:, 0:2], in_=carry[:, hc, :]),
                         reads=[b_carry], writes=[b_ab[a]])
                    P.op("act", lambda e, a=a, ia=ia: e.activation(out=abuf[a][:, 2:2 + T_], in_=ps[ia][:, :T_],
                                                                    func=AF.Identity),
                         reads=[b_ps[ia]], writes=[b_ab[a]])
                    if first:
                        P.op("dve", lambda e, a=a: e.tensor_scalar(out=abuf[a][:, 2:4], in0=abuf[a][:, 2:4],
                                                                    scalar1=vec[:, FLAG:FLAG + 1], scalar2=None, op0=ALU.mult),
                             reads=[b_ab[a], b_vec], writes=[b_ab[a]])
                    if halo:
                        P.op("dve", lambda e, a=a, hc=hc: e.tensor_scalar(
                            out=carry[:, hc, :], in0=abuf[a][:, T_:T_ + 2], scalar1=vec[:, FLAG:FLAG + 1],
                            scalar2=None, op0=ALU.mult), reads=[b_ab[a], b_vec], writes=[b_carry])
                        continue
                    P.op("dve", lambda e, a=a, hc=hc: e.tensor_copy(out=carry[:, hc, :], in_=abuf[a][:, T_:T_ + 2]),
                         reads=[b_ab[a]], writes=[b_carry])
                    ig = nxt("ps", NPS)
                    mm_group(ps[ig], b_ps[ig], T_, NFC, lambda k, s=s: wU[s][:, k * 256 + 128:k * 256 + 256],
                             lambda k: xn_t[:, k, :T_], [b_wU[s], b_xn])
                    P.op("act", lambda e, a=a, ia=ia, hc=hc: e.activation(
                        out=ybuf[a][:, :T_], in_=ps[ia][:, :T_], func=AF.Identity,
                        bias=vec[:, CB + hc:CB + hc + 1], scale=vec[:, CW + 88 + hc:CW + 88 + hc + 1]),
                        reads=[b_ps[ia], b_vec], writes=[b_y[a]])

                    P.op("dve", lambda e, a=a, hc=hc: e.scalar_tensor_tensor(
                        out=ybuf[a][:, :T_], in0=abuf[a][:, 1:1 + T_], scalar=vec[:, CW + 44 + hc:CW + 44 + hc + 1],
                        in1=ybuf[a][:, :T_], op0=ALU.mult, op1=ALU.add), reads=[b_ab[a], b_y[a], b_vec], writes=[b_y[a]])
                    P.op("dve", lambda e, a=a, hc=hc: e.scalar_tensor_tensor(
                        out=ybuf[a][:, :T_], in0=abuf[a][:, 0:T_], scalar=vec[:, CW + hc:CW + hc + 1],
                        in1=ybuf[a][:, :T_], op0=ALU.mult, op1=ALU.add), reads=[b_ab[a], b_y[a], b_vec], writes=[b_y[a]])
                    P.op("act", lambda e, a=a: e.activation(out=gbuf[a][:, :T_], in_=ybuf[a][:, :T_], func=AF.Gelu),
                         reads=[b_y[a]], writes=[b_g[a]])
                    P.op("dve", lambda e, a=a, ig=ig, hl=hl: e.tensor_tensor(
                        out=h_t[:, hl, :T_], in0=gbuf[a][:, :T_], in1=ps[ig][:, :T_], op=ALU.mult),
                        reads=[b_g[a], b_ps[ig]], writes=[b_h])
                if halo:
                    continue
                for oc in range(16):
                    s = nxt("wD", NWD)
                    P.op("sp", lambda e, s=s, oc=oc, half=half: e.dma_start(
                        out=wD[s][:], in_=wdnb[oc, :, half * 22 * 128:(half + 1) * 22 * 128]),
                        reads=[b_wcast[("wdn", oc)]], writes=[b_wD[s]], chan=c_wD[s])
                    i = nxt("ps", NPS)
                    mm_group(ps[i], b_ps[i], T_, 22, lambda k, s=s: wD[s][:, k * 128:(k + 1) * 128],
                             lambda k: h_t[:, k, :T_], [b_wD[s], b_h])
                    P.op("dve", lambda e, i=i, oc=oc: e.tensor_tensor(out=x_t[:, oc, :T_], in0=x_t[:, oc, :T_],
                                                                       in1=ps[i][:, :T_], op=ALU.add),
                         reads=[b_ps[i], b_x], writes=[b_x])
            if halo:
                return
            if stage == 3:
                P.op("sp", lambda e: e.dma_start(out=outT[:, :, c0 - 2:c0 - 2 + T_], in_=x_t[:, :, :T_]),
                     reads=[b_x], writes=[b_out], chan=c_out)
                return
            rmsnorm(T_, GP)
            P.op("sp", lambda e: e.dma_start(out=p_t[:, :, lo:T_], in_=pT[:, :, c0 + lo - 2:c0 - 2 + T_]),
                 writes=[b_p], chan=c_p)
            P.op("act", lambda e: e.activation(out=pb_t[:, :, :T_], in_=p_t[:, :, :T_], func=AF.Identity),
                 reads=[b_p], writes=[b_pb])
            for oc in range(16):
                s = nxt("wA", NWA)
                P.op("sp", lambda e, s=s, oc=oc: e.dma_start(out=wA[s][:], in_=wpgb[oc]),
                     reads=[b_wcast[("wpg", oc)]], writes=[b_wA[s]], chan=c_wA[s])
                i = nxt("ps", NPS)
                mm_group(ps[i], b_ps[i], T_, NFC, lambda k, s=s: wA[s][:, k * 128:(k + 1) * 128],
                         lambda k: xn_t[:, k, :T_], [b_wA[s], b_xn])
                j = nxt("ps", NPS)
                mm_group(ps[j], b_ps[j], T_, 2, lambda k, oc=oc: wppsb[:, k * D + oc * 128:k * D + (oc + 1) * 128],
                         lambda k: pb_t[:, k, :T_], [b_wpp, b_pb])
                a = nxt("ab", NA)
                P.op("act", lambda e, a=a, i=i: e.activation(out=sgbuf[a][:, :T_], in_=ps[i][:, :T_], func=AF.Sigmoid),
                     reads=[b_ps[i]], writes=[b_sg[a]])
                P.op("dve", lambda e, a=a, j=j: e.tensor_tensor(out=sgbuf[a][:, :T_], in0=sgbuf[a][:, :T_],
                                                                 in1=ps[j][:, :T_], op=ALU.mult),
                     reads=[b_sg[a], b_ps[j]], writes=[b_sg[a]])
                P.op("dve", lambda e, a=a, oc=oc: e.tensor_tensor(out=x_t[:, oc, :T_], in0=x_t[:, oc, :T_],
                                                                    in1=sgbuf[a][:, :T_], op=ALU.add),
                     reads=[b_sg[a], b_x], writes=[b_x])
            P.op("sp", lambda e: e.dma_start(out=outT[:, :, c0 + lo - 2:c0 - 2 + T_], in_=x_t[:, :, lo:T_]),
                 reads=[b_x], writes=[b_out], chan=c_out)

        ncols = NTOK + 2
        nt2 = -(-ncols // T)
        base, rem = divmod(ncols, nt2)
        bounds = []
        c = 0
        for it in range(nt2):
            w = base + (1 if it < rem else 0)
            bounds.append((c, w))
            c += w
        P.op("pool", lambda e: e.memset(p_t[:], 0.0), writes=[b_p])
        for it, (c0_, w_) in enumerate(bounds):
            tile(c0_, w_, False, first=(it == 0), nxtile=(bounds[it + 1] if it + 1 < nt2 else None))
        P.op("sp", lambda e: None, reads=[b_out])
        P.run()
    return nc


def mix_perm():
    perm = np.zeros(D, np.int64)
    for g in range(4):
        perm[g * 512:g * 512 + 256] = g * 256 + np.arange(256)
        perm[g * 512 + 256:(g + 1) * 512] = 1024 + g * 256 + np.arange(256)
    return perm


def fm(a, nch):
    n = a.shape[1]
    return np.ascontiguousarray(a.reshape(nch, 128, n).transpose(1, 0, 2))


def wtile(w, kin, cols):
    out = []
    for cc in cols:
        t = w[:, cc].reshape(kin, 128, len(cc)).transpose(1, 0, 2).reshape(128, kin * len(cc))
        out.append(t)
    return np.ascontiguousarray(np.stack(out, 0))


def phase2_weight_inputs(inp):
    perm = mix_perm()
    w_out = inp["w_out"][0][perm, :]
    w_up = inp["w_up"][0]
    ar = np.arange(128)
    d = {}
    d["wout"] = wtile(w_out, NFC, [oc * 128 + ar for oc in range(16)])
    d["wup"] = wtile(w_up, NFC, [np.concatenate([hc * 128 + ar, DFF + hc * 128 + ar]) for hc in range(NHC)])
    d["wdn"] = wtile(inp["w_down"][0], NHC, [oc * 128 + ar for oc in range(16)])
    d["wpg"] = wtile(inp["w_ple_gate"][0], NFC, [oc * 128 + ar for oc in range(16)])
    d["wpp"] = np.ascontiguousarray(inp["w_ple_proj"][0].reshape(2, 128, D).transpose(1, 0, 2).reshape(128, 2 * D))
    v = np.zeros((128, 16 + 16 + 44 * 4 + 1), np.float32)
    v[:, 0:16] = inp["ffn_norm_g"][0].reshape(16, 128).T
    v[:, 16:32] = inp["ple_norm_g"][0].reshape(16, 128).T
    cw = inp["conv_w"][0]
    for j in range(3):
        v[:, 32 + 44 * j:32 + 44 * (j + 1)] = cw[j].reshape(44, 128).T
    v[:, 32 + 132:32 + 176] = inp["conv_b"][0].reshape(44, 128).T
    d["vecs"] = v
    return d


def run_phase2(inp, mix_full, S):
    B = inp["x"].shape[0]
    NTOK = S // 4
    nc = build_phase2(NTOK)
    wd = phase2_weight_inputs(inp)
    in_maps = []
    for c in range(8):
        b, tq = c // 4, c % 4
        lo = tq * NTOK - 2
        xTb = inp["x"][b].T
        if tq == 0:
            mx = np.concatenate([np.zeros((D, 2), NPBF), mix_full[b][:, :NTOK]], 1)
            xx = np.concatenate([np.zeros((D, 2), np.float32), xTb[:, :NTOK]], 1)
        else:
            mx = mix_full[b][:, lo:lo + NTOK + 2]
            xx = xTb[:, lo:lo + NTOK + 2]
        m = dict(wd)
        m["vecs"] = wd["vecs"].copy()
        m["vecs"][:, -1] = 0.0 if tq == 0 else 1.0
        m["mixT"] = fm(np.ascontiguousarray(mx), NFC)
        m["xT"] = fm(np.ascontiguousarray(xx), NFC)
        m["pT"] = fm(np.ascontiguousarray(inp["p"][0, b, tq * NTOK:(tq + 1) * NTOK, :].T), 2)
        in_maps.append(m)
    res = run_bass_kernel_spmd(nc, in_maps, core_ids=list(range(8)))
    out = np.zeros((B, S, D), np.float32)
    for c in range(8):
        b, tq = c // 4, c % 4
        o = np.asarray(res.results[c]["outT"])
        out[b, tq * NTOK:(tq + 1) * NTOK, :] = o.transpose(2, 1, 0).reshape(NTOK, D)
    return out


NCONST = 11
FE_YIELDS = 4 * 17


def phase1_consts():
    s = np.arange(128)[:, None]
    t = np.arange(128)[None, :]
    same = (s // 64) == (t // 64)
    c = np.zeros((NCONST, 128, 128), np.float32)
    c[0] = (s == t)
    c[1] = -1.0 * ((s <= t) & same)
    c[2] = -1.0 * same
    c[3] = -1.0 * (s < 64) * np.ones_like(t)
    c[4] = -1.0 * (s >= 64) * np.ones_like(t)
    c[5] = -1.0 * (s <= t)
    c[6] = -1.0
    c[7] = 1.0 * (s == 64) * np.ones_like(t)
    c[8] = 1.0 * ((s <= t) & same)
    c[9] = 1.0 * (s <= t)
    c[10] = 1.0 * (s == 0) * np.ones_like(t)
    return np.ascontiguousarray(c.transpose(1, 0, 2).reshape(128, NCONST * 128))


def build_phase1(S):
    nc = bass.Bass("TRN2", target_bir_lowering=False)
    NT = S // 128
    din = lambda n, s, dt=F32: nc.dram_tensor(n, list(s), dt, kind="ExternalInput").ap()
    xT = din("xT", [NT, 128, D])
    wAin = din("wA", [NFC, 128, 1026])
    wBin = din("wB", [2, NFC, 128, 385])
    vecs = din("vecs_in", [128, 20])
    goutin = din("gout_in", [128, 256])
    gqkin = din("gqk_in", [128, 256])
    cin = din("consts", [128, NCONST * 128])
    mixo = nc.dram_tensor("mixo", [512, S], BF16, kind="ExternalOutput").ap()
    dbg = nc.dram_tensor("dbg", [128, 8192], F32, kind="ExternalOutput").ap() if DEBUG.get("dump") else None

    with ExitStack() as st:
        P = Prog(nc, st)
        B = lambda: Buf()
        W = sb(nc, st, "W", [128, NFC, 1026], BF16)
        wst = [sb(nc, st, "wst%d" % i, [128, 1026], F32) for i in range(2)]
        xf = [sb(nc, st, "xf%d" % i, [128, D], F32) for i in range(2)]
        xb = [sb(nc, st, "xb%d" % i, [128, D], BF16) for i in range(2)]
        xq = sb(nc, st, "xq", [128, D], BF16)
        vec = sb(nc, st, "vec", [128, 20], F32)
        gout = sb(nc, st, "gout", [128, 256], F32)
        gqk = sb(nc, st, "gqk", [128, 256], F32)
        cst = sb(nc, st, "cst", [128, NCONST * 128], F32)
        identb = sb(nc, st, "identb", [128, 128], BF16)
        onesb = sb(nc, st, "onesb", [128, 128], BF16)
        rtab = sb(nc, st, "rtab", [128, 3, NT], F32)
        sc = sb(nc, st, "sc", [128, 64], F32)
        junk = sb(nc, st, "junk", [128, 256], F32)
        qtm = sb(nc, st, "qtm", [128, 256], BF16)
        ktm = sb(nc, st, "ktm", [128, 256], BF16)
        kw = sb(nc, st, "kw", [128, 256], BF16)
        vaug = sb(nc, st, "vaug", [128, 257], BF16)
        sgo = sb(nc, st, "sgo", [128, 256], F32)
        QTa = sb(nc, st, "QTa", [128, 2, 128], BF16)
        QTb = sb(nc, st, "QTb", [128, 2, 128], BF16)
        KTm = sb(nc, st, "KTm", [128, 2, 128], BF16)
        sTm = sb(nc, st, "sTm", [128, 128], BF16)
        Cf = [sb(nc, st, "Cf%d" % i, [128, 2, 257], F32) for i in range(2)]
        Cb = [sb(nc, st, "Cb%d" % i, [128, 2, 257], BF16) for i in range(2)]
        hatm = sb(nc, st, "hatm", [128, 256], BF16)
        dtmp = sb(nc, st, "dtmp", [128, 2, 257], F32)
        b_dt = Buf()
        ostA = sb(nc, st, "ostA", [128, 2, 512], BF16)
        KT = sb(nc, st, "KT", [128, S], BF16)
        V = sb(nc, st, "V", [128, NT, 128], BF16)
        QT2 = sb(nc, st, "QT2", [128, 2, 512], BF16)
        qkh = sb(nc, st, "qkh", [128, 256], BF16)
        Fg = sb(nc, st, "Fg", [128, NT], F32)
        Fblk = sb(nc, st, "Fblk", [128, 1], F32)
        biasq2 = sb(nc, st, "biasq2", [128, 2, NT], F32)
        crow2 = sb(nc, st, "crow2", [128, 2, 512], BF16)
        e0b = sb(nc, st, "e0b", [128, 128], BF16)
        zrow = sb(nc, st, "zrow", [1, 128], F32)
        onesrow = sb(nc, st, "onesrow", [1, 128], BF16)
        acc = sb(nc, st, "acc", [128, 512], F32)
        NPT = 6
        PT = [sb(nc, st, "PT%d" % i, [128, 512], BF16) for i in range(NPT)]
        rec = sb(nc, st, "rec", [128, 512], F32)
        ostB = sb(nc, st, "ostB", [128, 512], BF16)
        z0 = ps_bank(nc, st, "z0")
        z1 = ps_bank(nc, st, "z1")
        z2 = ps_bank(nc, st, "z2")
        tpsb = ps_bank(nc, st, "tpsb", BF16, 1024)
        p4 = ps_bank(nc, st, "p4")
        p5 = ps_bank(nc, st, "p5")
        p6 = ps_bank(nc, st, "p6")
        p7 = ps_bank(nc, st, "p7")

        b_W, b_vec, b_cst, b_id, b_rt, b_xq, b_z, b_z2, b_tps, b_p4, b_p5, b_p6, b_p7 = [B() for _ in range(13)]
        b_wst = [B(), B()]
        b_xf = [B(), B()]
        b_xb = [B(), B()]
        b_out = B()
        c_wst = [P.chan(), P.chan()]
        c_xf = [P.chan(), P.chan()]
        c_misc = P.chan()
        c_oA = P.chan()
        c_oB = P.chan()
        C = lambda k: cst[:, k * 128:(k + 1) * 128]
        dstage = sb(nc, st, "dstage", [128, 520], F32)
        b_ds = B()
        c_dbg = P.chan()
        dump_off = {"n": 0}
        DEBUG["dump_offsets"] = {}

        def dump(name, ap, ncol, reads):
            if dbg is None:
                return
            off = dump_off["n"]
            dump_off["n"] += ncol
            DEBUG["dump_offsets"][name] = (off, ncol)
            P.op("act", lambda e: e.activation(out=dstage[:, :ncol], in_=ap, func=AF.Identity), reads=reads, writes=[b_ds])
            P.op("sp", lambda e: e.dma_start(out=dbg[:, off:off + ncol], in_=dstage[:, :ncol], allow_slow_non_contiguous=True), reads=[b_ds],
                 writes=[b_out], chan=c_dbg)

        for dst, src in ((vec, vecs), (gout, goutin), (gqk, gqkin), (cst, cin)):
            P.op("sp", lambda e, dst=dst, src=src: e.dma_start(out=dst[:], in_=src), writes=[b_cst], chan=c_misc)
        b_vec = b_cst
        P.op("pool", lambda e: e.memset(onesb[:], 1.0), writes=[b_id])
        P.op("pool", lambda e: e.memset(vaug[:], 1.0), writes=[b_id])
        P.op("pool", lambda e: e.memset(zrow[:], 0.0), writes=[b_id])
        P.op("pool", lambda e: e.memset(onesrow[:], 1.0), writes=[b_id])
        P.op("pool", lambda e: e.memset(crow2[:], 0.0), writes=[b_id])
        P.op("act", lambda e: e.activation(out=e0b[:], in_=C(10), func=AF.Identity), reads=[b_cst], writes=[b_id])
        P.op("pool", lambda e: e.memset(QTa[:], 0.0), writes=[b_id])
        P.op("pool", lambda e: e.memset(QTb[:], 0.0), writes=[b_id])
        P.op("act", lambda e: e.activation(out=identb[:], in_=C(0), func=AF.Identity), reads=[b_cst], writes=[b_id])
        P.op("act", lambda e: e.activation(out=gqk[:, 0:128], in_=gqk[:, 0:128], func=AF.Identity, scale=128.0 ** -0.5),
             reads=[b_cst], writes=[b_cst])
        cnt = {"x": 0, "w": 0}

        def load_weights(src, ncol):
            for fc in range(NFC):
                s = cnt["w"] % 2
                cnt["w"] += 1
                P.op("sp", lambda e, s=s, fc=fc: e.dma_start(out=wst[s][:, :ncol], in_=src[fc]),
                     writes=[b_wst[s]], chan=c_wst[s])
                P.op("act", lambda e, s=s, fc=fc: e.activation(out=W[:, fc, :ncol], in_=wst[s][:, :ncol], func=AF.Identity,
                                                               scale=vec[:, fc:fc + 1]),
                     reads=[b_wst[s], b_vec], writes=[b_W])

        SB = {}

        def bs(n):
            return SB.setdefault(n, Buf())
        b_rt2, b_zg, b_zs, b_zf, b_zr, b_junk, b_kt, b_dt2, b_Fblk, b_rt1 = [B() for _ in range(10)]
        ACT = lambda fn, reads, writes: P.op("act", fn, reads=reads, writes=writes)
        DVE = lambda fn, reads, writes: P.op("dve", fn, reads=reads, writes=writes)
        PE = lambda fn, reads, writes: P.op("pe", fn, reads=reads, writes=writes)
        AX = mybir.AxisListType.X

        def front(nt, groups, first):
            s = cnt["x"] % 2
            cnt["x"] += 1
            P.op("sp", lambda e: e.dma_start(out=xf[s][:], in_=xT[nt]), writes=[b_xf[s]], chan=c_xf[s])
            DVE(lambda e: e.tensor_copy(out=xb[s][:], in_=xf[s][:]), [b_xf[s]], [b_xb[s]])
            if first:
                ACT(lambda e: e.activation(out=xq[:], in_=xf[s][:], func=AF.Square), [b_xf[s]], [b_xq])

                def ssq(e):
                    ins = None
                    for fc in range(NFC):
                        ins = e.matmul(z2[:, 16:17], xq[:, fc * 128:(fc + 1) * 128], onesb[:, 0:1],
                                       start=(fc == 0), stop=(fc == NFC - 1))
                    return ins
                PE(ssq, [b_xq, b_id], [b_z2])
                ACT(lambda e: e.activation(out=sc[:, 0:1], in_=z2[:, 16:17], func=AF.Ln, bias=EPS, scale=1.0 / D),
                    [b_z2], [bs("rs")])
                ACT(lambda e: e.activation(out=rtab[:, 0, nt:nt + 1], in_=sc[:, 0:1], func=AF.Exp, scale=-0.5), [bs("rs")], [b_rt])
                ACT(lambda e: e.activation(out=rtab[:, 1, nt:nt + 1], in_=rtab[:, 0, nt:nt + 1], func=AF.Identity, scale=-1.0),
                    [b_rt], [b_rt1])
                DVE(lambda e: e.tensor_tensor(out=rtab[:, 2, nt:nt + 1], in0=rtab[:, 0, nt:nt + 1],
                                              in1=rtab[:, 0, nt:nt + 1], op=ALU.mult), [b_rt], [b_rt2])

            def proj(e):
                ins = None
                for fc in range(NFC):
                    for (zt, c0, w) in groups:
                        ins = e.matmul(zt[:, :w], xb[s][:, fc * 128:(fc + 1) * 128], W[:, fc, c0:c0 + w],
                                       start=(fc == 0), stop=(fc == NFC - 1))
                return ins
            PE(proj, [b_xb[s], b_W], [b_z, b_z2])

        b_q, b_k, b_v, b_go, b_qt, b_sT, b_kw, b_ha, b_oA = [B() for _ in range(9)]
        b_Cf = [B(), B()]
        b_Cb = [B(), B()]
        BI, NBF = 16, 17
        if not DEBUG.get("skipA"):
            load_weights(wAin, 1026)
            ACT(lambda e: e.activation(out=W[:, :, 0:256], in_=W[:, :, 0:256], func=AF.Identity, scale=1.0 / 16.0),
                [b_W], [b_W])
            P.op("pool", lambda e: e.memset(Cf[0][:], 0.0), writes=[b_Cf[0]])
            P.op("pool", lambda e: e.memset(Cb[0][:], 0.0), writes=[b_Cb[0]])
        rngA = range(DEBUG.get("startA", 0), DEBUG.get("ntA", NT) if not DEBUG.get("skipA") else 0)
        GA = [(z0, 0, 512), (z1, 512, 512), (z2, 1024, 2)]
        if len(rngA):
            front(rngA[0], GA, True)
        for nt in rngA:
            if DEBUG.get('stopA', 99) <= 1:
                continue
            r = rtab[:, 0, nt:nt + 1]
            ACT(lambda e, r=r: e.activation(out=qtm[:], in_=z0[:, 0:256], func=AF.Identity, scale=r), [b_z, b_rt], [b_q])
            ACT(lambda e, r=r: e.activation(out=ktm[:], in_=z0[:, 256:512], func=AF.Identity, scale=r), [b_z, b_rt], [b_k])
            ACT(lambda e, r=r: e.activation(out=vaug[:, 0:256], in_=z1[:, 0:256], func=AF.Identity, scale=r),
                [b_z, b_rt], [b_v])
            nr = rtab[:, 1, nt:nt + 1]
            ACT(lambda e, nr=nr: e.activation(out=sgo[:], in_=z1[:, 256:512], func=AF.Exp, scale=nr), [b_z, b_rt1], [b_go])
            ACT(lambda e, r=r: e.activation(out=sc[:, 3:4], in_=z2[:, 0:1], func=AF.Identity, scale=r,
                                            bias=vec[:, BI:BI + 1]), [b_z2, b_rt, b_vec], [bs("ig")])
            ACT(lambda e, r=r: e.activation(out=sc[:, 1:2], in_=z2[:, 1:2], func=AF.Identity, scale=r,
                                            bias=vec[:, NBF:NBF + 1]), [b_z2, b_rt, b_vec], [bs("y")])
            if nt + 1 < rngA[-1] + 1:
                front(nt + 1, GA, True)
            ACT(lambda e: e.activation(out=sgo[:], in_=sgo[:], func=AF.Ln, bias=1.0, scale=1.0), [b_go], [b_go])
            ACT(lambda e: e.activation(out=sgo[:], in_=sgo[:], func=AF.Exp, scale=-1.0), [b_go], [b_go])
            DVE(lambda e: e.tensor_tensor(out=sgo[:], in0=sgo[:], in1=gout[:], op=ALU.mult), [b_go, b_cst], [b_go])
            ACT(lambda e: e.activation(out=sc[:, 1:2], in_=sc[:, 1:2], func=AF.Exp, scale=-1.0), [bs("y")], [bs("y")])
            ACT(lambda e: e.activation(out=sc[:, 2:3], in_=sc[:, 1:2], func=AF.Ln, bias=1.0, scale=1.0), [bs("y")], [bs("l1p")])

            def gmm(e):
                ins = None
                for j in range(4):
                    ins = e.matmul(z2[:, 32 + j:33 + j], C(1 + j), sc[:, 2:3], start=True, stop=True)
                return ins
            PE(gmm, [bs("l1p"), b_cst], [b_zg])
            DVE(lambda e: e.tensor_copy(out=sc[:, 4:8], in_=z2[:, 32:36]), [b_zg], [bs("gsb")])
            DVE(lambda e: e.tensor_tensor(out=sc[:, 8:9], in0=sc[:, 3:4], in1=z2[:, 32:33], op=ALU.subtract),
                [bs("ig"), b_zg], [bs("imb")])

            def g4(e):
                e.activation(out=sc[:, 9:10], in_=sc[:, 8:9], func=AF.Exp)
                e.activation(out=sc[:, 10:11], in_=sc[:, 4:5], func=AF.Exp)
                e.activation(out=sc[:, 11:12], in_=sc[:, 8:9], func=AF.Exp, bias=sc[:, 5:6], scale=1.0)
                return e.activation(out=sc[:, 12:14], in_=sc[:, 6:8], func=AF.Exp)
            ACT(g4, [bs("imb"), bs("gsb")], [bs("exps")])
            if nt == DEBUG.get("dump_nt", -1):
                dump("r", rtab[:, 0, nt:nt + 1], 1, [b_rt])
                dump("qtm", qtm[:], 256, [b_q])
                dump("ktm", ktm[:], 256, [b_k])
                dump("vaug", vaug[:], 257, [b_v])
                dump("sgo", sgo[:], 256, [b_go])
                dump("sc", sc[:, 0:16], 16, [bs("exps"), bs("imb"), bs("gsb"), bs("ig"), bs("l1p"), bs("y")])

            if DEBUG.get('stopA', 99) <= 2:
                continue
            def tr(e):
                ins = None
                for kc in range(2):
                    e.transpose(tpsb[:, kc * 128:(kc + 1) * 128], qtm[:, kc * 128:(kc + 1) * 128], identb[:])
                    ins = e.transpose(tpsb[:, (2 + kc) * 128:(3 + kc) * 128], ktm[:, kc * 128:(kc + 1) * 128], identb[:])
                return ins
            PE(tr, [b_q, b_k, b_id], [b_tps])

            def ev1(e):
                ins = None
                for kc in range(2):
                    e.activation(out=QTa[:, kc, 0:64], in_=tpsb[:, kc * 128:kc * 128 + 64], func=AF.Identity)
                    ins = e.activation(out=QTb[:, kc, 64:128], in_=tpsb[:, kc * 128 + 64:(kc + 1) * 128], func=AF.Identity)
                return ins
            ACT(ev1, [b_tps], [b_qt])
            ACT(lambda e: e.activation(out=KTm[:].rearrange("p a b -> p (a b)"), in_=tpsb[:, 256:512], func=AF.Identity),
                [b_tps], [b_kt])

            def smm(e):
                ins = None
                for kc in range(2):
                    e.matmul(z2[:, 128:256], KTm[:, kc, :], QTa[:, kc, :], start=(kc == 0), stop=False)
                    ins = e.matmul(z2[:, 128:256], KTm[:, kc, :], QTb[:, kc, :], start=False, stop=(kc == 1))
                return ins
            PE(smm, [b_qt, b_kt], [b_zs])
            ACT(lambda e: e.activation(out=junk[:, 0:128], in_=z2[:, 128:256], func=AF.Identity, scale=sc[:, 9:10]),
                [b_zs, bs("exps")], [b_junk])
            DVE(lambda e: e.tensor_tensor(out=sTm[:], in0=junk[:, 0:128], in1=C(8), op=ALU.mult), [b_junk, b_cst], [b_sT])
            DVE(lambda e: e.tensor_scalar(out=kw[:], in0=ktm[:], scalar1=sc[:, 11:12], scalar2=None, op0=ALU.mult),
                [b_k, bs("exps")], [b_kw])

            if DEBUG.get('stopA', 99) <= 3:
                continue
            def upd(c, src, dst):
                def dmm(e):
                    e.matmul(p6[:, 0:257], kw[64 * c:64 * c + 64, 0:128], vaug[64 * c:64 * c + 64, :], start=True, stop=True)
                    return e.matmul(p7[:, 0:257], kw[64 * c:64 * c + 64, 128:256], vaug[64 * c:64 * c + 64, :],
                                    start=True, stop=True)
                PE(dmm, [b_kw, b_v], [b_p6])

                def cev(e):
                    e.activation(out=dtmp[:, 0, :], in_=p6[:, 0:257], func=AF.Identity)
                    return e.activation(out=dtmp[:, 1, :], in_=p7[:, 0:257], func=AF.Identity)
                ACT(cev, [b_p6], [b_dt])

                def cup(e):
                    e.scalar_tensor_tensor(out=Cf[dst][:, 0, :], in0=Cf[src][:, 0, :], scalar=sc[:, 12 + c:13 + c],
                                           in1=dtmp[:, 0, :], op0=ALU.mult, op1=ALU.add)
                    return e.scalar_tensor_tensor(out=Cf[dst][:, 1, :], in0=Cf[src][:, 1, :], scalar=sc[:, 12 + c:13 + c],
                                                  in1=dtmp[:, 1, :], op0=ALU.mult, op1=ALU.add)
                DVE(cup, [b_dt, b_Cf[src], bs("exps")], [b_Cf[dst]])
                ACT(lambda e: e.activation(out=Cb[dst][:], in_=Cf[dst][:], func=AF.Identity), [b_Cf[dst]], [b_Cb[dst]])
            if nt == DEBUG.get("dump_nt", -1):
                dump("QTa", QTa[:].rearrange("p a b -> p (a b)"), 256, [b_qt])
                dump("QTb", QTb[:].rearrange("p a b -> p (a b)"), 256, [b_qt])
                dump("KTm", KTm[:].rearrange("p a b -> p (a b)"), 256, [b_kt])
                dump("sTm", sTm[:], 128, [b_sT])
                dump("kw", kw[:], 256, [b_kw])
                dump("C0", Cf[0][:].rearrange("p a b -> p (a b)"), 514, [b_Cf[0]])
            upd(0, 0, 1)
            if nt == DEBUG.get("dump_nt", -1):
                dump("C1", Cf[1][:].rearrange("p a b -> p (a b)"), 514, [b_Cf[1]])
                dump("C1b", Cb[1][:].rearrange("p a b -> p (a b)"), 514, [b_Cb[1]])

            def nmm(e):
                ins = None
                e.matmul(p5[:, 0:257], sTm[:], vaug[:], start=True, stop=False)
                for kc in range(2):
                    e.matmul(p5[:, 0:257], QTa[:, kc, :], Cb[0][:, kc, :], start=False, stop=False)
                    ins = e.matmul(p5[:, 0:257], QTb[:, kc, :], Cb[1][:, kc, :], start=False, stop=(kc == 1))
                return ins
            PE(nmm, [b_sT, b_v, b_qt, b_Cb[0], b_Cb[1]], [b_p5])
            if nt == DEBUG.get("dump_nt", -1):
                dump("num", p5[:, 0:257], 257, [b_p5])
            upd(1, 1, 0)
            if DEBUG.get('stopA', 99) <= 4:
                continue
            ACT(lambda e: e.activation(out=sc[:, 16:17], in_=p5[:, 256:257], func=AF.Abs, scale=sc[:, 10:11]),
                [b_p5, bs("exps")], [bs("dd")])
            DVE(lambda e: e.tensor_scalar_max(out=sc[:, 17:18], in0=sc[:, 16:17], scalar1=1.0), [bs("dd")], [bs("ad")])
            DVE(lambda e: e.reciprocal(out=sc[:, 17:18], in_=sc[:, 17:18]), [bs("ad")], [bs("ad")])
            DVE(lambda e: e.tensor_tensor(out=sc[:, 18:19], in0=sc[:, 10:11], in1=sc[:, 17:18], op=ALU.mult),
                [bs("ad"), bs("exps")], [bs("A")])
            DVE(lambda e: e.tensor_tensor(out=sc[:, 20:21], in0=sc[:, 18:19], in1=sc[:, 18:19], op=ALU.mult),
                [bs("A")], [bs("a2")])
            ACT(lambda e: e.activation(out=junk[:], in_=p5[:, 0:256], func=AF.Square, scale=1.0 / 16.0), [b_p5], [b_junk])
            DVE(lambda e: e.reduce_sum(out=sc[:, 19:20], in_=junk[:], axis=AX), [b_junk], [bs("ssq")])
            ACT(lambda e: e.activation(out=sc[:, 21:22], in_=sc[:, 19:20], func=AF.Ln, bias=EPS, scale=sc[:, 20:21]),
                [bs("ssq"), bs("a2")], [bs("sr")])
            ACT(lambda e: e.activation(out=sc[:, 21:22], in_=sc[:, 21:22], func=AF.Exp, scale=-0.5), [bs("sr")], [bs("sr")])
            DVE(lambda e: e.tensor_tensor(out=sc[:, 22:23], in0=sc[:, 21:22], in1=sc[:, 18:19], op=ALU.mult),
                [bs("sr"), bs("A")], [bs("scl")])
            ACT(lambda e: e.activation(out=junk[:], in_=p5[:, 0:256], func=AF.Identity, scale=sc[:, 22:23]),
                [b_p5, bs("scl")], [b_junk])
            DVE(lambda e: e.tensor_tensor(out=hatm[:], in0=junk[:], in1=sgo[:], op=ALU.mult), [b_junk, b_go], [b_ha])
            if nt == DEBUG.get("dump_nt", -1):
                dump("sc2", sc[:, 16:24], 8, [bs("scl"), bs("sr"), bs("A"), bs("a2"), bs("ssq"), bs("ad"), bs("dd")])
                dump("hatm", hatm[:], 256, [b_ha])

            if DEBUG.get('stopA', 99) <= 5:
                continue
            def otr(e):
                e.transpose(tpsb[:, 512:640], hatm[:, 0:128], identb[:])
                return e.transpose(tpsb[:, 640:768], hatm[:, 128:256], identb[:])
            PE(otr, [b_ha, b_id], [b_tps])
            j = nt % 4

            def oev(e, j=j):
                e.activation(out=ostA[:, 0, j * 128:(j + 1) * 128], in_=tpsb[:, 512:640], func=AF.Identity)
                return e.activation(out=ostA[:, 1, j * 128:(j + 1) * 128], in_=tpsb[:, 640:768], func=AF.Identity)
            ACT(oev, [b_tps], [b_oA])
            if j == 3:
                t0 = (nt - 3) * 128
                P.op("sp", lambda e, t0=t0: e.dma_start(out=mixo[0:256, t0:t0 + 512].rearrange("(c p) t -> p c t", p=128),
                                                        in_=ostA[:]), reads=[b_oA], writes=[b_out], chan=c_oA)

        b_qk, b_F, b_oB, b_rec, b_acc, b_p7 = [B() for _ in range(6)]
        b_QT = [B(), B()]
        b_crow = [B(), B()]
        b_b4 = [B(), B()]
        b_KT = [B() for _ in range(NT)]
        b_V = [B() for _ in range(NT)]
        b_st = [B(), B()]
        b_PT = [B() for _ in range(NPT)]
        stt = [p4, p5]

        def tile_gen(kb, hh, firstB):
            NBX = 18 + hh
            r = rtab[:, 0, kb:kb + 1]
            r2 = rtab[:, 2, kb:kb + 1]
            j = kb % 4
            par = (kb // 4) % 2
            cR0 = 33 + par
            cdR = 36 + 4 * par + j
            ACT(lambda e: e.activation(out=junk[:], in_=z0[:, 0:256], func=AF.Square, scale=128.0 ** -0.5), [b_z], [b_junk])
            yield
            DVE(lambda e: e.reduce_sum(out=sc[:, 24:26], in_=junk[:].rearrange("p (a b) -> p a b", a=2), axis=AX),
                [b_junk], [bs("ssq2")])
            yield
            ACT(lambda e: e.activation(out=sc[:, 26:28], in_=sc[:, 24:26], func=AF.Ln, bias=EPS, scale=r2),
                [bs("ssq2"), b_rt2], [bs("s2")])
            yield
            ACT(lambda e: e.activation(out=sc[:, 26:28], in_=sc[:, 26:28], func=AF.Exp, scale=-0.5), [bs("s2")], [bs("s2")])
            yield
            DVE(lambda e: e.tensor_scalar(out=sc[:, 28:30], in0=sc[:, 26:28], scalar1=r, scalar2=None, op0=ALU.mult),
                [bs("s2"), b_rt], [bs("sc2")])
            yield

            def s2b(e):
                e.activation(out=junk[:, 0:128], in_=z0[:, 0:128], func=AF.Identity, scale=sc[:, 28:29])
                return e.activation(out=junk[:, 128:256], in_=z0[:, 128:256], func=AF.Identity, scale=sc[:, 29:30])
            ACT(s2b, [bs("sc2"), b_z], [b_junk])
            ACT(lambda e: e.activation(out=V[:, kb, :], in_=z0[:, 256:384], func=AF.Identity, scale=r), [b_z, b_rt], [b_V[kb]])
            ACT(lambda e: e.activation(out=sc[:, 30:31], in_=z0[:, 384:385], func=AF.Identity, scale=r,
                                       bias=vec[:, NBX:NBX + 1]), [b_z, b_rt, b_vec], [bs("yB")])
            yield
            if kb + 1 < NT:
                front(kb + 1, [(z0, 0, 385)], firstB)
            yield
            DVE(lambda e: e.tensor_tensor(out=qkh[:], in0=junk[:], in1=gqk[:], op=ALU.mult), [b_junk, b_cst], [b_qk])
            ACT(lambda e: e.activation(out=sc[:, 30:31], in_=sc[:, 30:31], func=AF.Exp, scale=-1.0), [bs("yB")], [bs("yB")])
            yield
            ACT(lambda e: e.activation(out=sc[:, 31:32], in_=sc[:, 30:31], func=AF.Ln, bias=1.0, scale=1.0), [bs("yB")], [bs("l1pB")])

            def tr2(e):
                e.transpose(tpsb[:, 0:128], qkh[:, 0:128], identb[:])
                return e.transpose(tpsb[:, 128:256], qkh[:, 128:256], identb[:])
            PE(tr2, [b_qk, b_id], [b_tps])
            yield
            ACT(lambda e: e.activation(out=QT2[:, par, j * 128:(j + 1) * 128], in_=tpsb[:, 0:128], func=AF.Identity),
                [b_tps], [b_QT[par]])
            ACT(lambda e: e.activation(out=KT[:, kb * 128:(kb + 1) * 128], in_=tpsb[:, 128:256], func=AF.Identity),
                [b_tps], [b_KT[kb]])

            def fm1(e):
                e.matmul(z2[:, 40:41], C(5), sc[:, 31:32], start=True, stop=True)
                return e.matmul(z2[:, 41:42], C(6), sc[:, 31:32], start=True, stop=True)
            PE(fm1, [bs("l1pB"), b_cst], [b_zf])
            yield
            DVE(lambda e: e.tensor_tensor(out=Fg[:, kb:kb + 1], in0=Fblk[:], in1=z2[:, 40:41], op=ALU.add), [b_zf, b_Fblk], [b_F])
            DVE(lambda e: e.tensor_tensor(out=Fblk[:], in0=Fblk[:], in1=z2[:, 41:42], op=ALU.add), [b_zf, b_Fblk], [b_Fblk])
            yield
            PE(lambda e: e.matmul(z2[:, 48:49], C(7), Fg[:, kb:kb + 1], start=True, stop=True), [b_F, b_cst], [b_zr])
            yield
            DVE(lambda e: e.tensor_copy(out=sc[:, 32:33], in_=z2[:, 48:49]), [b_zr], [bs("Rb")])
            yield
            if j == 0:
                DVE(lambda e: e.tensor_copy(out=sc[:, cR0:cR0 + 1], in_=sc[:, 32:33]), [bs("Rb")], [bs("Rb0_%d" % par)])
                yield
            DVE(lambda e: e.tensor_tensor(out=sc[:, cdR:cdR + 1], in0=sc[:, 32:33], in1=sc[:, cR0:cR0 + 1], op=ALU.subtract),
                [bs("Rb"), bs("Rb0_%d" % par)], [bs("dR%d" % cdR)])
            yield
            ACT(lambda e: e.activation(out=crow2[0:1, par, j * 128:(j + 1) * 128], in_=zrow[0:1, :], func=AF.Identity,
                                       bias=sc[0:1, cdR:cdR + 1], scale=1.0), [bs("dR%d" % cdR), b_id], [b_crow[par]])
            yield
            if j == 3:
                nkb = kb + 1
                DVE(lambda e: e.tensor_scalar(out=biasq2[:, par, 0:nkb], in0=Fg[:, 0:nkb], scalar1=sc[:, cR0:cR0 + 1],
                                              scalar2=cst[:, 6 * 128:6 * 128 + 1], op0=ALU.subtract, op1=ALU.mult),
                    [b_F, bs("Rb0_%d" % par), b_cst], [b_b4[par]])
                yield

        def attn_gen(Q, hh):
            par = Q % 2
            nkb = 4 * Q + 4

            def qk(kbp):
                c0 = 128 * max(0, kbp - 4 * Q)
                si = kbp % 2

                def qkmm(e):
                    e.matmul(stt[si][:, c0:512], KT[:, kbp * 128:(kbp + 1) * 128], QT2[:, par, c0:512], start=True, stop=False)
                    return e.matmul(stt[si][:, c0:512], e0b[:], crow2[:, par, c0:512], start=False, stop=True)
                PE(qkmm, [b_KT[kbp], b_QT[par], b_crow[par], b_id], [b_st[si]])
            qk(0)
            for kbp in range(nkb):
                c0 = 128 * max(0, kbp - 4 * Q)
                si = kbp % 2
                pi = kbp % NPT
                ACT(lambda e, c0=c0, si=si, pi=pi, kbp=kbp: e.activation(
                    out=PT[pi][:, c0:512], in_=stt[si][:, c0:512], func=AF.Exp, bias=biasq2[:, par, kbp:kbp + 1], scale=1.0),
                    [b_st[si], b_b4[par]], [b_PT[pi]])
                if kbp >= 4 * Q:
                    DVE(lambda e, pi=pi, c0=c0: e.tensor_tensor(out=PT[pi][:, c0:c0 + 128], in0=PT[pi][:, c0:c0 + 128],
                                                                in1=C(9), op=ALU.mult), [b_PT[pi], b_cst], [b_PT[pi]])
                if kbp + 1 < nkb:
                    qk(kbp + 1)
                PE(lambda e, pi=pi, c0=c0, kbp=kbp: e.matmul(
                    p6[:, c0:512], V[:, kbp, :], PT[pi][:, c0:512], start=(kbp == 0), stop=(kbp == nkb - 1)),
                    [b_V[kbp], b_PT[pi]], [b_p6])
                if kbp == 0:
                    DVE(lambda e, pi=pi: e.tensor_copy(out=acc[:], in_=PT[pi][:]), [b_PT[pi]], [b_acc])
                else:
                    DVE(lambda e, pi=pi, c0=c0: e.tensor_tensor(out=acc[:, c0:512], in0=acc[:, c0:512], in1=PT[pi][:, c0:512],
                                                                op=ALU.add), [b_PT[pi], b_acc], [b_acc])
                yield
            PE(lambda e: e.matmul(p7[:], C(6), acc[:], start=True, stop=True), [b_acc, b_cst], [b_p7])
            ACT(lambda e: e.activation(out=rec[:], in_=p7[:], func=AF.Identity, scale=-1.0), [b_p7], [b_rec])
            DVE(lambda e: e.reciprocal(out=rec[:], in_=rec[:]), [b_rec], [b_rec])
            DVE(lambda e: e.tensor_tensor(out=ostB[:], in0=rec[:], in1=p6[:], op=ALU.mult), [b_p6, b_rec], [b_oB])
            r0 = 256 + hh * 128
            P.op("sp", lambda e: e.dma_start(out=mixo[r0:r0 + 128, Q * 512:(Q + 1) * 512], in_=ostB[:]),
                 reads=[b_oB], writes=[b_out], chan=c_oB)
            yield

        def fe_gen(Q, hh, firstB):
            for kb in range(4 * Q, 4 * Q + 4):
                yield from tile_gen(kb, hh, firstB)

        for hh in range(2 if not DEBUG.get("skipB") else 0):
            load_weights(wBin[hh], 385)
            P.op("pool", lambda e: e.memset(Fblk[:], 0.0), writes=[b_Fblk])
            firstB = bool(DEBUG.get("skipA")) and hh == 0
            front(0, [(z0, 0, 385)], firstB)
            pending = None
            for Q in range(NT // 4):
                fe = fe_gen(Q, hh, firstB)
                if pending is not None and not DEBUG.get("nointerleave"):
                    nsteps = 4 * Q
                    per = max(1, -(-FE_YIELDS // max(1, nsteps)))
                    for _ in pending:
                        for _ in range(per):
                            next(fe, None)
                elif pending is not None:
                    for _ in pending:
                        pass
                for _ in fe:
                    pass
                pending = attn_gen(Q, hh)
            for _ in pending:
                pass
        P.op("sp", lambda e: None, reads=[b_out])
        P.run()
    return nc


def phase1_inputs(inp, S):
    w_in = inp["w_in"][0]
    x = inp["x"]
    NT = S // 128
    consts = phase1_consts()
    maps = []
    xTs = []
    for b in range(2):
        xTs.append(np.ascontiguousarray(x[b].reshape(NT, 128, NFC, 128).transpose(0, 3, 2, 1).reshape(NT, 128, D)))
    gb = inp["mlstm_gate_bias"][0]
    fb = inp["fox_f_bias"][0]
    ar = np.arange
    for c in range(8):
        b, g = c // 4, c % 4
        colsA = np.concatenate([g * 256 + ar(256), 1024 + g * 256 + ar(256), 2048 + g * 256 + ar(256),
                                3072 + g * 256 + ar(256), [4096 + g], [4100 + g]])
        wA = np.ascontiguousarray(w_in[:, colsA].reshape(NFC, 128, 1026))
        wB = []
        for hh in range(2):
            h = 2 * g + hh
            colsB = np.concatenate([4104 + h * 128 + ar(128), 5128 + h * 128 + ar(128), 6152 + h * 128 + ar(128),
                                    [7176 + h]])
            wB.append(w_in[:, colsB].reshape(NFC, 128, 385))
        wB = np.ascontiguousarray(np.stack(wB, 0))
        v = np.zeros((128, 20), np.float32)
        v[:, 0:16] = inp["mix_norm_g"][0].reshape(16, 128).T
        v[:, 16] = gb[g]
        v[:, 17] = gb[4 + g]
        v[:, 18] = fb[2 * g]
        v[:, 19] = fb[2 * g + 1]
        gout = np.ascontiguousarray(np.broadcast_to(inp["mlstm_out_g"][0][g * 256:(g + 1) * 256], (128, 256)))
        gqk = np.ascontiguousarray(np.broadcast_to(np.concatenate([inp["fox_q_g"][0], inp["fox_k_g"][0]]), (128, 256)))
        maps.append({"xT": xTs[b], "wA": wA, "wB": wB, "vecs_in": v, "gout_in": gout, "gqk_in": gqk, "consts": consts})
    return maps


def run_phase1(inp, S):
    nc = build_phase1(S)
    res = run_bass_kernel_spmd(nc, phase1_inputs(inp, S), core_ids=list(range(8)))
    mix = np.zeros((2, D, S), NPBF)
    for c in range(8):
        b, g = c // 4, c % 4
        mix[b, g * 512:(g + 1) * 512, :] = np.asarray(res.results[c]["mixo"])
    if DEBUG.get("dump"):
        DEBUG["dbg"] = [np.asarray(res.results[c]["dbg"]) for c in range(8)]
    return mix


def kernel(**inp):
    inp = {k: np.asarray(v) for k, v in inp.items()}
    S = inp["x"].shape[1]
    mix = run_phase1(inp, S)
    return run_phase2(inp, mix, S)
```

```python
import numpy as np
import ml_dtypes
from contextlib import ExitStack
import concourse.bass as bass
import concourse.mybir as mybir
from concourse.bass_utils import run_bass_kernel_spmd

F32 = mybir.dt.float32
BF16 = mybir.dt.bfloat16
AF = mybir.ActivationFunctionType
ALU = mybir.AluOpType
NPBF = ml_dtypes.bfloat16

D = 2048
NFC = 16
DFF = 5632
NHC = 44
PLE = 256
EPS = 1e-6
DEBUG = {}


class Buf:
    __slots__ = ("lw", "rd")

    def __init__(self):
        self.lw = None
        self.rd = {}


class Chan:
    def __init__(self, sem):
        self.sem = sem
        self.n = 0


class Op:
    __slots__ = ("eng", "fn", "deps", "signal", "count", "chan")


class Prog:
    CE = ("pe", "act", "dve", "pool")

    def __init__(self, nc, stack):
        self.nc = nc
        self.stack = stack
        self.ops = {e: [] for e in self.CE + ("sp",)}
        self.sem = {e: stack.enter_context(nc.semaphore("sem_" + e)) for e in self.CE}
        self.nchan = 0

    def chan(self):
        self.nchan += 1
        return Chan(self.stack.enter_context(self.nc.semaphore("ch%d" % self.nchan)))

    def op(self, eng, fn, reads=(), writes=(), chan=None):
        o = Op()
        o.eng = eng
        o.fn = fn
        o.signal = False
        o.count = 0
        o.chan = chan
        deps = {}
        raw = set()
        for b in reads:
            if b.lw is not None:
                deps[id(b.lw)] = b.lw
                raw.add(id(b.lw))
        for b in writes:
            if b.lw is not None:
                deps[id(b.lw)] = b.lw
            for r in b.rd.values():
                deps[id(r)] = r
        o.deps = []
        for d in deps.values():
            if d is o:
                continue
            if d.chan is not None or d.eng != eng or chan is not None or (id(d) in raw and eng != "pe"):
                o.deps.append(d)
                if d.chan is None:
                    d.signal = True
        if chan is not None:
            chan.n += 16
            o.count = chan.n
        key = chan if chan is not None else eng
        for b in reads:
            b.rd[id(key)] = o
        for b in writes:
            b.lw = o
            b.rd = {}
        self.ops[eng].append(o)
        return o

    def finalize(self):
        for e in self.CE:
            c = 0
            for o in self.ops[e]:
                if o.signal:
                    c += 1
                    o.count = c

    def emit(self, eng, e):
        seen = {}
        for o in self.ops[eng]:
            for d in o.deps:
                s = d.chan.sem if d.chan is not None else self.sem[d.eng]
                k = id(s)
                if seen.get(k, 0) >= d.count:
                    continue
                seen[k] = d.count
                e.wait_ge(s, d.count)
            ins = o.fn(e)
            if ins is None:
                continue
            if o.chan is not None:
                ins.then_inc(o.chan.sem, 16)
            elif o.signal:
                ins.then_inc(self.sem[eng], 1)

    def run(self):
        self.finalize()
        with self.nc.Block() as block:
            @block.tensor
            def _(e):
                self.emit("pe", e)

            @block.scalar
            def _(e):
                self.emit("act", e)

            @block.vector
            def _(e):
                self.emit("dve", e)

            @block.gpsimd
            def _(e):
                self.emit("pool", e)

            @block.sync
            def _(e):
                self.emit("sp", e)


def sb(nc, stack, name, shape, dt):
    return stack.enter_context(nc.sbuf_tensor(name, list(shape), dt))


def ps_bank(nc, stack, name, dt=F32, n=512):
    return stack.enter_context(nc.psum_tensor(name, [128, n], dt))


def build_phase2(NTOK, T=512):
    nc = bass.Bass("TRN2", target_bir_lowering=False)
    NC2 = NTOK + 2
    ntile = NTOK // T
    din = lambda n, s, dt=F32: nc.dram_tensor(n, list(s), dt, kind="ExternalInput").ap()
    mixT = din("mixT", [128, NFC, NC2], BF16)
    xT = din("xT", [128, NFC, NC2])
    pT = din("pT", [128, 2, NTOK])
    wout = din("wout", [16, 128, NFC * 128])
    wup = din("wup", [NHC, 128, NFC * 256])
    wdn = din("wdn", [16, 128, NHC * 128])
    wpg = din("wpg", [16, 128, NFC * 128])
    wpp = din("wpp", [128, 2 * D])
    vecs = din("vecs", [128, 16 + 16 + 44 * 4 + 1])
    outT = nc.dram_tensor("outT", [128, NFC, NTOK], F32, kind="ExternalOutput").ap()
    woutb = nc.dram_tensor("woutb", [16, 128, NFC * 128], BF16).ap()
    wupb = nc.dram_tensor("wupb", [NHC, 128, NFC * 256], BF16).ap()
    wdnb = nc.dram_tensor("wdnb", [16, 128, NHC * 128], BF16).ap()
    wpgb = nc.dram_tensor("wpgb", [16, 128, NFC * 128], BF16).ap()
    wppb = nc.dram_tensor("wppb", [128, 2 * D], BF16).ap()

    with ExitStack() as st:
        P = Prog(nc, st)
        x_t = sb(nc, st, "x_t", [128, NFC, T], F32)
        mix_t = sb(nc, st, "mix_t", [128, NFC, T], BF16)
        sq_t = sb(nc, st, "sq_t", [128, NFC, T], BF16)
        xn_t = sb(nc, st, "xn_t", [128, NFC, T], BF16)
        h_t = sb(nc, st, "h_t", [128, 22, T], BF16)
        p_t = sb(nc, st, "p_t", [128, 2, T], F32)
        pb_t = sb(nc, st, "pb_t", [128, 2, T], BF16)
        rs_t = sb(nc, st, "rs_t", [128, T], F32)
        NA = 2
        abuf = [sb(nc, st, "abuf%d" % i, [128, T + 2], F32) for i in range(NA)]
        ybuf = [sb(nc, st, "ybuf%d" % i, [128, T], F32) for i in range(NA)]
        gbuf = [sb(nc, st, "gbuf%d" % i, [128, T], F32) for i in range(NA)]
        sgbuf = [sb(nc, st, "sgbuf%d" % i, [128, T], F32) for i in range(NA)]
        carry = sb(nc, st, "carry", [128, NHC, 2], F32)
        vec = sb(nc, st, "vec", [128, 16 + 16 + 44 * 4 + 1], F32)
        ones = sb(nc, st, "ones", [128, 128], BF16)
        wppsb = sb(nc, st, "wppsb", [128, 2 * D], BF16)
        NWA, NWU, NWD = 2, 3, 2
        wA = [sb(nc, st, "wA%d" % i, [128, NFC * 128], BF16) for i in range(NWA)]
        wU = [sb(nc, st, "wU%d" % i, [128, NFC * 256], BF16) for i in range(NWU)]
        wD = [sb(nc, st, "wD%d" % i, [128, 22 * 128], BF16) for i in range(NWD)]
        NPS = 8
        ps = [ps_bank(nc, st, "ps%d" % i) for i in range(NPS)]

        B = lambda: Buf()
        b_x, b_mix, b_sq, b_xn, b_h, b_p, b_pb, b_rs, b_carry, b_vec, b_ones, b_wpp = [B() for _ in range(12)]
        b_ab = [B() for _ in range(NA)]
        b_y = [B() for _ in range(NA)]
        b_g = [B() for _ in range(NA)]
        b_sg = [B() for _ in range(NA)]
        b_wA = [B() for _ in range(NWA)]
        b_wU = [B() for _ in range(NWU)]
        b_wD = [B() for _ in range(NWD)]
        b_ps = [B() for _ in range(NPS)]
        b_out = B()
        c_x, c_mix, c_p, c_vec, c_wpp, c_out = [P.chan() for _ in range(6)]
        c_wA = [P.chan() for _ in range(NWA)]
        c_wU = [P.chan() for _ in range(NWU)]
        c_wD = [P.chan() for _ in range(NWD)]
        cnt = {"ps": 0, "wA": 0, "wU": 0, "wD": 0, "ab": 0}

        def nxt(k, n):
            i = cnt[k] % n
            cnt[k] += 1
            return i

        P.op("sp", lambda e: e.dma_start(out=vec[:], in_=vecs), writes=[b_vec], chan=c_vec)
        P.op("pool", lambda e: e.memset(ones[:], 1.0), writes=[b_ones])
        P.op("pool", lambda e: e.memset(carry[:], 0.0), writes=[b_carry])
        b_wcast = {}

        def cast(name, dst, src, pieces):
            n = dst.shape[0]
            step = (n + pieces - 1) // pieces
            for i in range(0, n, step):
                bb = B()
                ch = P.chan()
                P.op("pool", lambda e, i=i: e.dma_start(out=dst[i:i + step], in_=src[i:i + step],
                                                         max_dma_last_dim=4096), writes=[bb], chan=ch)
                for j in range(i, min(n, i + step)):
                    b_wcast[(name, j)] = bb

        cast("wout", woutb, wout, 2)
        cast("wup", wupb, wup, 11)
        cast("wdn", wdnb, wdn, 4)
        cast("wpg", wpgb, wpg, 2)
        bb = B()
        P.op("pool", lambda e: e.dma_start(out=wppb, in_=wpp, max_dma_last_dim=4096), writes=[bb], chan=P.chan())
        P.op("sp", lambda e: e.dma_start(out=wppsb[:], in_=wppb), reads=[bb], writes=[b_wpp], chan=c_wpp)

        GF, GP, CW, CB, FLAG = 0, 16, 32, 32 + 132, 32 + 176

        def mm_group(pst, bps, T_, n, lhs, rhs, rbufs):
            def fn(e):
                ins = None
                for k in range(n):
                    ins = e.matmul(pst[:, :T_], lhs(k), rhs(k), start=(k == 0), stop=(k == n - 1))
                return ins
            P.op("pe", fn, reads=rbufs, writes=[bps])

        def rmsnorm(T_, goff):
            P.op("act", lambda e: e.activation(out=sq_t[:, :, :T_], in_=x_t[:, :, :T_], func=AF.Square),
                 reads=[b_x], writes=[b_sq])
            i = nxt("ps", NPS)
            mm_group(ps[i], b_ps[i], T_, NFC, lambda k: ones[:], lambda k: sq_t[:, k, :T_], [b_ones, b_sq])
            P.op("act", lambda e: e.activation(out=rs_t[:, :T_], in_=ps[i][:, :T_], func=AF.Sqrt,
                                               bias=EPS, scale=1.0 / D), reads=[b_ps[i]], writes=[b_rs])
            P.op("dve", lambda e: e.reciprocal(out=rs_t[:, :T_], in_=rs_t[:, :T_]), reads=[b_rs], writes=[b_rs])

            def fn(e):
                ins = None
                for oc in range(NFC):
                    ins = e.scalar_tensor_tensor(out=xn_t[:, oc, :T_], in0=x_t[:, oc, :T_],
                                                 scalar=vec[:, goff + oc:goff + oc + 1], in1=rs_t[:, :T_],
                                                 op0=ALU.mult, op1=ALU.mult)
                return ins
            P.op("dve", fn, reads=[b_x, b_rs, b_vec], writes=[b_xn])

        def tile(c0, T_, halo, first=False, nxtile=None):
            lo = 2 if first else 0
            if first:
                P.op("sp", lambda e: e.dma_start(out=mix_t[:, :, :T_], in_=mixT[:, :, c0:c0 + T_]),
                     writes=[b_mix], chan=c_mix)
            P.op("sp", lambda e: e.dma_start(out=x_t[:, :, :T_], in_=xT[:, :, c0:c0 + T_]),
                 writes=[b_x], chan=c_x)
            stage = DEBUG.get("stage", 9)
            if stage == 0:
                if not halo:
                    P.op("sp", lambda e: e.dma_start(out=outT[:, :, c0 - 2:c0 - 2 + T_], in_=x_t[:, :, :T_]),
                         reads=[b_x], writes=[b_out], chan=c_out)
                return
            for oc in range(16):
                s = nxt("wA", NWA)
                P.op("sp", lambda e, s=s, oc=oc: e.dma_start(out=wA[s][:], in_=woutb[oc]),
                     reads=[b_wcast[("wout", oc)]], writes=[b_wA[s]], chan=c_wA[s])
                i = nxt("ps", NPS)
                mm_group(ps[i], b_ps[i], T_, NFC, lambda k, s=s: wA[s][:, k * 128:(k + 1) * 128],
                         lambda k: mix_t[:, k, :T_], [b_wA[s], b_mix])
                P.op("dve", lambda e, i=i, oc=oc: e.tensor_tensor(out=x_t[:, oc, :T_], in0=x_t[:, oc, :T_],
                                                                   in1=ps[i][:, :T_], op=ALU.add),
                     reads=[b_ps[i], b_x], writes=[b_x])
            if nxtile is not None:
                P.op("sp", lambda e: e.dma_start(out=mix_t[:, :, :nxtile[1]], in_=mixT[:, :, nxtile[0]:nxtile[0] + nxtile[1]]),
                     writes=[b_mix], chan=c_mix)
            if stage == 1:
                if not halo:
                    P.op("sp", lambda e: e.dma_start(out=outT[:, :, c0 - 2:c0 - 2 + T_], in_=x_t[:, :, :T_]),
                         reads=[b_x], writes=[b_out], chan=c_out)
                return
            rmsnorm(T_, GF)
            if stage == 2:
                if not halo:
                    P.op("act", lambda e: e.activation(out=x_t[:, :, :T_], in_=xn_t[:, :, :T_], func=AF.Identity),
                         reads=[b_xn], writes=[b_x])
                    P.op("sp", lambda e: e.dma_start(out=outT[:, :, c0 - 2:c0 - 2 + T_], in_=x_t[:, :, :T_]),
                         reads=[b_x], writes=[b_out], chan=c_out)
                return
            for half in range(2):
                for hl in range(22):
                    hc = half * 22 + hl
                    s = nxt("wU", NWU)
                    P.op("sp", lambda e, s=s, hc=hc: e.dma_start(out=wU[s][:], in_=wupb[hc]),
                         reads=[b_wcast[("wup", hc)]], writes=[b_wU[s]], chan=c_wU[s])
                    ia = nxt("ps", NPS)
                    mm_group(ps[ia], b_ps[ia], T_, NFC, lambda k, s=s: wU[s][:, k * 256:k * 256 + 128],
                             lambda k: xn_t[:, k, :T_], [b_wU[s], b_xn])
                    a = nxt("ab", NA)
                    P.op("dve", lambda e, a=a, hc=hc: e.tensor_copy(out=abuf[a][:, 0:2], in_=carry[:, hc, :]),
                         reads=[b_carry], writes=[b_ab[a]])
                    P.op("act", lambda e, a=a, ia=ia: e.activation(out=abuf[a][:, 2:2 + T_], in_=ps[ia][:, :T_],
                                                                    func=AF.Identity),
                         reads=[b_ps[ia]], writes=[b_ab[a]])
                    if first:
                        P.op("dve", lambda e, a=a: e.tensor_scalar(out=abuf[a][:, 2:4], in0=abuf[a][:, 2:4],
                                                                    scalar1=vec[:, FLAG:FLAG + 1], scalar2=None, op0=ALU.mult),
                             reads=[b_ab[a], b_vec], writes=[b_ab[a]])
                    if halo:
                        P.op("dve", lambda e, a=a, hc=hc: e.tensor_scalar(
                            out=carry[:, hc, :], in0=abuf[a][:, T_:T_ + 2], scalar1=vec[:, FLAG:FLAG + 1],
                            scalar2=None, op0=ALU.mult), reads=[b_ab[a], b_vec], writes=[b_carry])
                        continue
                    P.op("dve", lambda e, a=a, hc=hc: e.tensor_copy(out=carry[:, hc, :], in_=abuf[a][:, T_:T_ + 2]),
                         reads=[b_ab[a]], writes=[b_carry])
                    ig = nxt("ps", NPS)
                    mm_group(ps[ig], b_ps[ig], T_, NFC, lambda k, s=s: wU[s][:, k * 256 + 128:k * 256 + 256],
                             lambda k: xn_t[:, k, :T_], [b_wU[s], b_xn])
                    P.op("act", lambda e, a=a, ia=ia, hc=hc: e.activation(
                        out=ybuf[a][:, :T_], in_=ps[ia][:, :T_], func=AF.Identity,
                        bias=vec[:, CB + hc:CB + hc + 1], scale=vec[:, CW + 88 + hc:CW + 88 + hc + 1]),
                        reads=[b_ps[ia], b_vec], writes=[b_y[a]])

                    P.op("dve", lambda e, a=a, hc=hc: e.scalar_tensor_tensor(
                        out=ybuf[a][:, :T_], in0=abuf[a][:, 1:1 + T_], scalar=vec[:, CW + 44 + hc:CW + 44 + hc + 1],
                        in1=ybuf[a][:, :T_], op0=ALU.mult, op1=ALU.add), reads=[b_ab[a], b_y[a], b_vec], writes=[b_y[a]])
                    P.op("dve", lambda e, a=a, hc=hc: e.scalar_tensor_tensor(
                        out=ybuf[a][:, :T_], in0=abuf[a][:, 0:T_], scalar=vec[:, CW + hc:CW + hc + 1],
                        in1=ybuf[a][:, :T_], op0=ALU.mult, op1=ALU.add), reads=[b_ab[a], b_y[a], b_vec], writes=[b_y[a]])
                    P.op("act", lambda e, a=a: e.activation(out=gbuf[a][:, :T_], in_=ybuf[a][:, :T_], func=AF.Gelu),
                         reads=[b_y[a]], writes=[b_g[a]])
                    P.op("dve", lambda e, a=a, ig=ig, hl=hl: e.tensor_tensor(
                        out=h_t[:, hl, :T_], in0=gbuf[a][:, :T_], in1=ps[ig][:, :T_], op=ALU.mult),
                        reads=[b_g[a], b_ps[ig]], writes=[b_h])
                if halo:
                    continue
                for oc in range(16):
                    s = nxt("wD", NWD)
                    P.op("sp", lambda e, s=s, oc=oc, half=half: e.dma_start(
                        out=wD[s][:], in_=wdnb[oc, :, half * 22 * 128:(half + 1) * 22 * 128]),
                        reads=[b_wcast[("wdn", oc)]], writes=[b_wD[s]], chan=c_wD[s])
                    i = nxt("ps", NPS)
                    mm_group(ps[i], b_ps[i], T_, 22, lambda k, s=s: wD[s][:, k * 128:(k + 1) * 128],
                             lambda k: h_t[:, k, :T_], [b_wD[s], b_h])
                    P.op("dve", lambda e, i=i, oc=oc: e.tensor_tensor(out=x_t[:, oc, :T_], in0=x_t[:, oc, :T_],
                                                                       in1=ps[i][:, :T_], op=ALU.add),
                         reads=[b_ps[i], b_x], writes=[b_x])
            if halo:
                return
            if stage == 3:
                P.op("sp", lambda e: e.dma_start(out=outT[:, :, c0 - 2:c0 - 2 + T_], in_=x_t[:, :, :T_]),
                     reads=[b_x], writes=[b_out], chan=c_out)
                return
            rmsnorm(T_, GP)
            P.op("sp", lambda e: e.dma_start(out=p_t[:, :, lo:T_], in_=pT[:, :, c0 + lo - 2:c0 - 2 + T_]),
                 writes=[b_p], chan=c_p)
            P.op("act", lambda e: e.activation(out=pb_t[:, :, :T_], in_=p_t[:, :, :T_], func=AF.Identity),
                 reads=[b_p], writes=[b_pb])
            for oc in range(16):
                s = nxt("wA", NWA)
                P.op("sp", lambda e, s=s, oc=oc: e.dma_start(out=wA[s][:], in_=wpgb[oc]),
                     reads=[b_wcast[("wpg", oc)]], writes=[b_wA[s]], chan=c_wA[s])
                i = nxt("ps", NPS)
                mm_group(ps[i], b_ps[i], T_, NFC, lambda k, s=s: wA[s][:, k * 128:(k + 1) * 128],
                         lambda k: xn_t[:, k, :T_], [b_wA[s], b_xn])
                j = nxt("ps", NPS)
                mm_group(ps[j], b_ps[j], T_, 2, lambda k, oc=oc: wppsb[:, k * D + oc * 128:k * D + (oc + 1) * 128],
                         lambda k: pb_t[:, k, :T_], [b_wpp, b_pb])
                a = nxt("ab", NA)
                P.op("act", lambda e, a=a, i=i: e.activation(out=sgbuf[a][:, :T_], in_=ps[i][:, :T_], func=AF.Sigmoid),
                     reads=[b_ps[i]], writes=[b_sg[a]])
                P.op("dve", lambda e, a=a, j=j: e.tensor_tensor(out=sgbuf[a][:, :T_], in0=sgbuf[a][:, :T_],
                                                                 in1=ps[j][:, :T_], op=ALU.mult),
                     reads=[b_sg[a], b_ps[j]], writes=[b_sg[a]])
                P.op("dve", lambda e, a=a, oc=oc: e.tensor_tensor(out=x_t[:, oc, :T_], in0=x_t[:, oc, :T_],
                                                                    in1=sgbuf[a][:, :T_], op=ALU.add),
                     reads=[b_sg[a], b_x], writes=[b_x])
            P.op("sp", lambda e: e.dma_start(out=outT[:, :, c0 + lo - 2:c0 - 2 + T_], in_=x_t[:, :, lo:T_]),
                 reads=[b_x], writes=[b_out], chan=c_out)

        ncols = NTOK + 2
        nt2 = -(-ncols // T)
        base, rem = divmod(ncols, nt2)
        bounds = []
        c = 0
        for it in range(nt2):
            w = base + (1 if it < rem else 0)
            bounds.append((c, w))
            c += w
        P.op("pool", lambda e: e.memset(p_t[:], 0.0), writes=[b_p])
        for it, (c0_, w_) in enumerate(bounds):
            tile(c0_, w_, False, first=(it == 0), nxtile=(bounds[it + 1] if it + 1 < nt2 else None))
        P.op("sp", lambda e: None, reads=[b_out])
        P.run()
    return nc


def mix_perm():
    perm = np.zeros(D, np.int64)
    for g in range(4):
        perm[g * 512:g * 512 + 256] = g * 256 + np.arange(256)
        perm[g * 512 + 256:(g + 1) * 512] = 1024 + g * 256 + np.arange(256)
    return perm


def fm(a, nch):
    n = a.shape[1]
    return np.ascontiguousarray(a.reshape(nch, 128, n).transpose(1, 0, 2))


def wtile(w, kin, cols):
    out = []
    for cc in cols:
        t = w[:, cc].reshape(kin, 128, len(cc)).transpose(1, 0, 2).reshape(128, kin * len(cc))
        out.append(t)
    return np.ascontiguousarray(np.stack(out, 0))


def phase2_weight_inputs(inp):
    perm = mix_perm()
    w_out = inp["w_out"][0][perm, :]
    w_up = inp["w_up"][0]
    ar = np.arange(128)
    d = {}
    d["wout"] = wtile(w_out, NFC, [oc * 128 + ar for oc in range(16)])
    d["wup"] = wtile(w_up, NFC, [np.concatenate([hc * 128 + ar, DFF + hc * 128 + ar]) for hc in range(NHC)])
    d["wdn"] = wtile(inp["w_down"][0], NHC, [oc * 128 + ar for oc in range(16)])
    d["wpg"] = wtile(inp["w_ple_gate"][0], NFC, [oc * 128 + ar for oc in range(16)])
    d["wpp"] = np.ascontiguousarray(inp["w_ple_proj"][0].reshape(2, 128, D).transpose(1, 0, 2).reshape(128, 2 * D))
    v = np.zeros((128, 16 + 16 + 44 * 4 + 1), np.float32)
    v[:, 0:16] = inp["ffn_norm_g"][0].reshape(16, 128).T
    v[:, 16:32] = inp["ple_norm_g"][0].reshape(16, 128).T
    cw = inp["conv_w"][0]
    for j in range(3):
        v[:, 32 + 44 * j:32 + 44 * (j + 1)] = cw[j].reshape(44, 128).T
    v[:, 32 + 132:32 + 176] = inp["conv_b"][0].reshape(44, 128).T
    d["vecs"] = v
    return d


def run_phase2(inp, mix_full, S):
    B = inp["x"].shape[0]
    NTOK = S // 4
    nc = build_phase2(NTOK)
    wd = phase2_weight_inputs(inp)
    in_maps = []
    for c in range(8):
        b, tq = c // 4, c % 4
        lo = tq * NTOK - 2
        xTb = inp["x"][b].T
        if tq == 0:
            mx = np.concatenate([np.zeros((D, 2), NPBF), mix_full[b][:, :NTOK]], 1)
            xx = np.concatenate([np.zeros((D, 2), np.float32), xTb[:, :NTOK]], 1)
        else:
            mx = mix_full[b][:, lo:lo + NTOK + 2]
            xx = xTb[:, lo:lo + NTOK + 2]
        m = dict(wd)
        m["vecs"] = wd["vecs"].copy()
        m["vecs"][:, -1] = 0.0 if tq == 0 else 1.0
        m["mixT"] = fm(np.ascontiguousarray(mx), NFC)
        m["xT"] = fm(np.ascontiguousarray(xx), NFC)
        m["pT"] = fm(np.ascontiguousarray(inp["p"][0, b, tq * NTOK:(tq + 1) * NTOK, :].T), 2)
        in_maps.append(m)
    res = run_bass_kernel_spmd(nc, in_maps, core_ids=list(range(8)))
    out = np.zeros((B, S, D), np.float32)
    for c in range(8):
        b, tq = c // 4, c % 4
        o = np.asarray(res.results[c]["outT"])
        out[b, tq * NTOK:(tq + 1) * NTOK, :] = o.transpose(2, 1, 0).reshape(NTOK, D)
    return out


NCONST = 11
FE_YIELDS = 4 * 17


def phase1_consts():
    s = np.arange(128)[:, None]
    t = np.arange(128)[None, :]
    same = (s // 64) == (t // 64)
    c = np.zeros((NCONST, 128, 128), np.float32)
    c[0] = (s == t)
    c[1] = -1.0 * ((s <= t) & same)
    c[2] = -1.0 * same
    c[3] = -1.0 * (s < 64) * np.ones_like(t)
    c[4] = -1.0 * (s >= 64) * np.ones_like(t)
    c[5] = -1.0 * (s <= t)
    c[6] = -1.0
    c[7] = 1.0 * (s == 64) * np.ones_like(t)
    c[8] = 1.0 * ((s <= t) & same)
    c[9] = 1.0 * (s <= t)
    c[10] = 1.0 * (s == 0) * np.ones_like(t)
    return np.ascontiguousarray(c.transpose(1, 0, 2).reshape(128, NCONST * 128))


def build_phase1(S):
    nc = bass.Bass("TRN2", target_bir_lowering=False)
    NT = S // 128
    din = lambda n, s, dt=F32: nc.dram_tensor(n, list(s), dt, kind="ExternalInput").ap()
    xT = din("xT", [NT, 128, D])
    wAin = din("wA", [NFC, 128, 1026])
    wBin = din("wB", [2, NFC, 128, 385])
    vecs = din("vecs_in", [128, 20])
    goutin = din("gout_in", [128, 256])
    gqkin = din("gqk_in", [128, 256])
    cin = din("consts", [128, NCONST * 128])
    mixo = nc.dram_tensor("mixo", [512, S], BF16, kind="ExternalOutput").ap()
    dbg = nc.dram_tensor("dbg", [128, 8192], F32, kind="ExternalOutput").ap() if DEBUG.get("dump") else None

    with ExitStack() as st:
        P = Prog(nc, st)
        B = lambda: Buf()
        W = sb(nc, st, "W", [128, NFC, 1026], BF16)
        wst = [sb(nc, st, "wst%d" % i, [128, 1026], F32) for i in range(2)]
        xf = [sb(nc, st, "xf%d" % i, [128, D], F32) for i in range(2)]
        xb = [sb(nc, st, "xb%d" % i, [128, D], BF16) for i in range(2)]
        xq = sb(nc, st, "xq", [128, D], BF16)
        vec = sb(nc, st, "vec", [128, 20], F32)
        gout = sb(nc, st, "gout", [128, 256], F32)
        gqk = sb(nc, st, "gqk", [128, 256], F32)
        cst = sb(nc, st, "cst", [128, NCONST * 128], F32)
        identb = sb(nc, st, "identb", [128, 128], BF16)
        onesb = sb(nc, st, "onesb", [128, 128], BF16)
        rtab = sb(nc, st, "rtab", [128, 3, NT], F32)
        sc = sb(nc, st, "sc", [128, 64], F32)
        junk = sb(nc, st, "junk", [128, 256], F32)
        qtm = sb(nc, st, "qtm", [128, 256], BF16)
        ktm = sb(nc, st, "ktm", [128, 256], BF16)
        kw = sb(nc, st, "kw", [128, 256], BF16)
        vaug = sb(nc, st, "vaug", [128, 257], BF16)
        sgo = sb(nc, st, "sgo", [128, 256], F32)
        QTa = sb(nc, st, "QTa", [128, 2, 128], BF16)
        QTb = sb(nc, st, "QTb", [128, 2, 128], BF16)
        KTm = sb(nc, st, "KTm", [128, 2, 128], BF16)
        sTm = sb(nc, st, "sTm", [128, 128], BF16)
        Cf = [sb(nc, st, "Cf%d" % i, [128, 2, 257], F32) for i in range(2)]
        Cb = [sb(nc, st, "Cb%d" % i, [128, 2, 257], BF16) for i in range(2)]
        hatm = sb(nc, st, "hatm", [128, 256], BF16)
        dtmp = sb(nc, st, "dtmp", [128, 2, 257], F32)
        b_dt = Buf()
        ostA = sb(nc, st, "ostA", [128, 2, 512], BF16)
        KT = sb(nc, st, "KT", [128, S], BF16)
        V = sb(nc, st, "V", [128, NT, 128], BF16)
        QT2 = sb(nc, st, "QT2", [128, 2, 512], BF16)
        qkh = sb(nc, st, "qkh", [128, 256], BF16)
        Fg = sb(nc, st, "Fg", [128, NT], F32)
        Fblk = sb(nc, st, "Fblk", [128, 1], F32)
        biasq2 = sb(nc, st, "biasq2", [128, 2, NT], F32)
        crow2 = sb(nc, st, "crow2", [128, 2, 512], BF16)
        e0b = sb(nc, st, "e0b", [128, 128], BF16)
        zrow = sb(nc, st, "zrow", [1, 128], F32)
        onesrow = sb(nc, st, "onesrow", [1, 128], BF16)
        acc = sb(nc, st, "acc", [128, 512], F32)
        PT = [sb(nc, st, "PT%d" % i, [128, 512], BF16) for i in range(3)]
        rec = sb(nc, st, "rec", [128, 512], F32)
        ostB = sb(nc, st, "ostB", [128, 512], BF16)
        z0 = ps_bank(nc, st, "z0")
        z1 = ps_bank(nc, st, "z1")
        z2 = ps_bank(nc, st, "z2")
        tpsb = ps_bank(nc, st, "tpsb", BF16, 1024)
        p4 = ps_bank(nc, st, "p4")
        p5 = ps_bank(nc, st, "p5")
        p6 = ps_bank(nc, st, "p6")
        p7 = ps_bank(nc, st, "p7")

        b_W, b_vec, b_cst, b_id, b_rt, b_xq, b_z, b_z2, b_tps, b_p4, b_p5, b_p6, b_p7 = [B() for _ in range(13)]
        b_wst = [B(), B()]
        b_xf = [B(), B()]
        b_xb = [B(), B()]
        b_out = B()
        c_wst = [P.chan(), P.chan()]
        c_xf = [P.chan(), P.chan()]
        c_misc = P.chan()
        c_oA = P.chan()
        c_oB = P.chan()
        C = lambda k: cst[:, k * 128:(k + 1) * 128]
        dstage = sb(nc, st, "dstage", [128, 520], F32)
        b_ds = B()
        c_dbg = P.chan()
        dump_off = {"n": 0}
        DEBUG["dump_offsets"] = {}

        def dump(name, ap, ncol, reads):
            if dbg is None:
                return
            off = dump_off["n"]
            dump_off["n"] += ncol
            DEBUG["dump_offsets"][name] = (off, ncol)
            P.op("act", lambda e: e.activation(out=dstage[:, :ncol], in_=ap, func=AF.Identity), reads=reads, writes=[b_ds])
            P.op("sp", lambda e: e.dma_start(out=dbg[:, off:off + ncol], in_=dstage[:, :ncol], allow_slow_non_contiguous=True), reads=[b_ds],
                 writes=[b_out], chan=c_dbg)

        for dst, src in ((vec, vecs), (gout, goutin), (gqk, gqkin), (cst, cin)):
            P.op("sp", lambda e, dst=dst, src=src: e.dma_start(out=dst[:], in_=src), writes=[b_cst], chan=c_misc)
        b_vec = b_cst
        P.op("pool", lambda e: e.memset(onesb[:], 1.0), writes=[b_id])
        P.op("pool", lambda e: e.memset(vaug[:], 1.0), writes=[b_id])
        P.op("pool", lambda e: e.memset(zrow[:], 0.0), writes=[b_id])
        P.op("pool", lambda e: e.memset(onesrow[:], 1.0), writes=[b_id])
        P.op("pool", lambda e: e.memset(crow2[:], 0.0), writes=[b_id])
        P.op("act", lambda e: e.activation(out=e0b[:], in_=C(10), func=AF.Identity), reads=[b_cst], writes=[b_id])
        P.op("pool", lambda e: e.memset(QTa[:], 0.0), writes=[b_id])
        P.op("pool", lambda e: e.memset(QTb[:], 0.0), writes=[b_id])
        P.op("act", lambda e: e.activation(out=identb[:], in_=C(0), func=AF.Identity), reads=[b_cst], writes=[b_id])
        P.op("act", lambda e: e.activation(out=gqk[:, 0:128], in_=gqk[:, 0:128], func=AF.Identity, scale=128.0 ** -0.5),
             reads=[b_cst], writes=[b_cst])
        cnt = {"x": 0, "w": 0}

        def load_weights(src, ncol):
            for fc in range(NFC):
                s = cnt["w"] % 2
                cnt["w"] += 1
                P.op("sp", lambda e, s=s, fc=fc: e.dma_start(out=wst[s][:, :ncol], in_=src[fc]),
                     writes=[b_wst[s]], chan=c_wst[s])
                P.op("act", lambda e, s=s, fc=fc: e.activation(out=W[:, fc, :ncol], in_=wst[s][:, :ncol], func=AF.Identity,
                                                               scale=vec[:, fc:fc + 1]),
                     reads=[b_wst[s], b_vec], writes=[b_W])

        SB = {}

        def bs(n):
            return SB.setdefault(n, Buf())
        b_rt2, b_junk, b_kt, b_dt2, b_Fblk, b_rt1 = [B() for _ in range(6)]
        b_zg = b_zs = b_zf = b_zr = b_z2
        ACT = lambda fn, reads, writes: P.op("act", fn, reads=reads, writes=writes)
        DVE = lambda fn, reads, writes: P.op("dve", fn, reads=reads, writes=writes)
        PE = lambda fn, reads, writes: P.op("pe", fn, reads=reads, writes=writes)
        AX = mybir.AxisListType.X

        def front(nt, groups, first):
            s = cnt["x"] % 2
            cnt["x"] += 1
            P.op("sp", lambda e: e.dma_start(out=xf[s][:], in_=xT[nt]), writes=[b_xf[s]], chan=c_xf[s])
            DVE(lambda e: e.tensor_copy(out=xb[s][:], in_=xf[s][:]), [b_xf[s]], [b_xb[s]])
            if first:
                ACT(lambda e: e.activation(out=xq[:], in_=xf[s][:], func=AF.Square), [b_xf[s]], [b_xq])

                def ssq(e):
                    ins = None
                    for fc in range(NFC):
                        ins = e.matmul(z2[:, 16:17], xq[:, fc * 128:(fc + 1) * 128], onesb[:, 0:1],
                                       start=(fc == 0), stop=(fc == NFC - 1))
                    return ins
                PE(ssq, [b_xq, b_id], [b_z2])
                ACT(lambda e: e.activation(out=sc[:, 0:1], in_=z2[:, 16:17], func=AF.Ln, bias=EPS, scale=1.0 / D),
                    [b_z2], [bs("rs")])
                ACT(lambda e: e.activation(out=rtab[:, 0, nt:nt + 1], in_=sc[:, 0:1], func=AF.Exp, scale=-0.5), [bs("rs")], [b_rt])
                ACT(lambda e: e.activation(out=rtab[:, 1, nt:nt + 1], in_=rtab[:, 0, nt:nt + 1], func=AF.Identity, scale=-1.0),
                    [b_rt], [b_rt1])
                DVE(lambda e: e.tensor_tensor(out=rtab[:, 2, nt:nt + 1], in0=rtab[:, 0, nt:nt + 1],
                                              in1=rtab[:, 0, nt:nt + 1], op=ALU.mult), [b_rt], [b_rt2])

            def proj(e):
                ins = None
                for fc in range(NFC):
                    for (zt, c0, w) in groups:
                        ins = e.matmul(zt[:, :w], xb[s][:, fc * 128:(fc + 1) * 128], W[:, fc, c0:c0 + w],
                                       start=(fc == 0), stop=(fc == NFC - 1))
                return ins
            PE(proj, [b_xb[s], b_W], [b_z, b_z2])

        b_q, b_k, b_v, b_go, b_qt, b_sT, b_kw, b_ha, b_oA = [B() for _ in range(9)]
        b_p7 = B()
        b_Cf = [B(), B()]
        b_Cb = [B(), B()]
        BI, NBF = 16, 17
        if not DEBUG.get("skipA"):
            load_weights(wAin, 1026)
            ACT(lambda e: e.activation(out=W[:, :, 0:256], in_=W[:, :, 0:256], func=AF.Identity, scale=1.0 / 16.0),
                [b_W], [b_W])
            P.op("pool", lambda e: e.memset(Cf[0][:], 0.0), writes=[b_Cf[0]])
            P.op("pool", lambda e: e.memset(Cb[0][:], 0.0), writes=[b_Cb[0]])
        rngA = range(DEBUG.get("startA", 0), DEBUG.get("ntA", NT) if not DEBUG.get("skipA") else 0)
        GA = [(z0, 0, 512), (z1, 512, 512), (z2, 1024, 2)]
        if len(rngA):
            front(rngA[0], GA, True)
        for nt in rngA:
            if DEBUG.get('stopA', 99) <= 1:
                continue
            r = rtab[:, 0, nt:nt + 1]
            ACT(lambda e, r=r: e.activation(out=qtm[:], in_=z0[:, 0:256], func=AF.Identity, scale=r), [b_z, b_rt], [b_q])
            ACT(lambda e, r=r: e.activation(out=ktm[:], in_=z0[:, 256:512], func=AF.Identity, scale=r), [b_z, b_rt], [b_k])
            ACT(lambda e, r=r: e.activation(out=vaug[:, 0:256], in_=z1[:, 0:256], func=AF.Identity, scale=r),
                [b_z, b_rt], [b_v])
            nr = rtab[:, 1, nt:nt + 1]
            ACT(lambda e, nr=nr: e.activation(out=sgo[:], in_=z1[:, 256:512], func=AF.Exp, scale=nr), [b_z, b_rt1], [b_go])
            ACT(lambda e, r=r: e.activation(out=sc[:, 3:4], in_=z2[:, 0:1], func=AF.Identity, scale=r,
                                            bias=vec[:, BI:BI + 1]), [b_z2, b_rt, b_vec], [bs("ig")])
            ACT(lambda e, r=r: e.activation(out=sc[:, 1:2], in_=z2[:, 1:2], func=AF.Identity, scale=r,
                                            bias=vec[:, NBF:NBF + 1]), [b_z2, b_rt, b_vec], [bs("y")])
            if nt + 1 < rngA[-1] + 1:
                front(nt + 1, GA, True)
            ACT(lambda e: e.activation(out=sgo[:], in_=sgo[:], func=AF.Ln, bias=1.0, scale=1.0), [b_go], [b_go])
            ACT(lambda e: e.activation(out=sgo[:], in_=sgo[:], func=AF.Exp, scale=-1.0), [b_go], [b_go])
            DVE(lambda e: e.tensor_tensor(out=sgo[:], in0=sgo[:], in1=gout[:], op=ALU.mult), [b_go, b_cst], [b_go])
            ACT(lambda e: e.activation(out=sc[:, 1:2], in_=sc[:, 1:2], func=AF.Exp, scale=-1.0), [bs("y")], [bs("y")])
            ACT(lambda e: e.activation(out=sc[:, 2:3], in_=sc[:, 1:2], func=AF.Ln, bias=1.0, scale=1.0), [bs("y")], [bs("l1p")])

            def gmm(e):
                ins = None
                for j in range(4):
                    ins = e.matmul(z2[:, 32 + j:33 + j], C(1 + j), sc[:, 2:3], start=True, stop=True)
                return ins
            PE(gmm, [bs("l1p"), b_cst], [b_zg])
            DVE(lambda e: e.tensor_copy(out=sc[:, 4:8], in_=z2[:, 32:36]), [b_zg], [bs("gsb")])
            DVE(lambda e: e.tensor_tensor(out=sc[:, 8:9], in0=sc[:, 3:4], in1=z2[:, 32:33], op=ALU.subtract),
                [bs("ig"), b_zg], [bs("imb")])

            def g4(e):
                e.activation(out=sc[:, 9:10], in_=sc[:, 8:9], func=AF.Exp)
                e.activation(out=sc[:, 10:11], in_=sc[:, 4:5], func=AF.Exp)
                e.activation(out=sc[:, 11:12], in_=sc[:, 8:9], func=AF.Exp, bias=sc[:, 5:6], scale=1.0)
                return e.activation(out=sc[:, 12:14], in_=sc[:, 6:8], func=AF.Exp)
            ACT(g4, [bs("imb"), bs("gsb")], [bs("exps")])
            if nt == DEBUG.get("dump_nt", -1):
                dump("r", rtab[:, 0, nt:nt + 1], 1, [b_rt])
                dump("qtm", qtm[:], 256, [b_q])
                dump("ktm", ktm[:], 256, [b_k])
                dump("vaug", vaug[:], 257, [b_v])
                dump("sgo", sgo[:], 256, [b_go])
                dump("sc", sc[:, 0:16], 16, [bs("exps"), bs("imb"), bs("gsb"), bs("ig"), bs("l1p"), bs("y")])

            if DEBUG.get('stopA', 99) <= 2:
                continue
            def tr(e):
                ins = None
                for kc in range(2):
                    e.transpose(tpsb[:, kc * 128:(kc + 1) * 128], qtm[:, kc * 128:(kc + 1) * 128], identb[:])
                    ins = e.transpose(tpsb[:, (2 + kc) * 128:(3 + kc) * 128], ktm[:, kc * 128:(kc + 1) * 128], identb[:])
                return ins
            PE(tr, [b_q, b_k, b_id], [b_tps])

            def ev1(e):
                ins = None
                for kc in range(2):
                    e.activation(out=QTa[:, kc, 0:64], in_=tpsb[:, kc * 128:kc * 128 + 64], func=AF.Identity)
                    ins = e.activation(out=QTb[:, kc, 64:128], in_=tpsb[:, kc * 128 + 64:(kc + 1) * 128], func=AF.Identity)
                return ins
            ACT(ev1, [b_tps], [b_qt])
            ACT(lambda e: e.activation(out=KTm[:].rearrange("p a b -> p (a b)"), in_=tpsb[:, 256:512], func=AF.Identity),
                [b_tps], [b_kt])

            def smm(e):
                ins = None
                for kc in range(2):
                    e.matmul(z2[:, 128:256], KTm[:, kc, :], QTa[:, kc, :], start=(kc == 0), stop=False)
                    ins = e.matmul(z2[:, 128:256], KTm[:, kc, :], QTb[:, kc, :], start=False, stop=(kc == 1))
                return ins
            PE(smm, [b_qt, b_kt], [b_zs])
            ACT(lambda e: e.activation(out=junk[:, 0:128], in_=z2[:, 128:256], func=AF.Identity, scale=sc[:, 9:10]),
                [b_zs, bs("exps")], [b_junk])
            DVE(lambda e: e.tensor_tensor(out=sTm[:], in0=junk[:, 0:128], in1=C(8), op=ALU.mult), [b_junk, b_cst], [b_sT])
            DVE(lambda e: e.tensor_scalar(out=kw[:], in0=ktm[:], scalar1=sc[:, 11:12], scalar2=None, op0=ALU.mult),
                [b_k, bs("exps")], [b_kw])

            if DEBUG.get('stopA', 99) <= 3:
                continue
            def upd(c, src, dst):
                def dmm(e):
                    e.matmul(p6[:, 0:257], kw[64 * c:64 * c + 64, 0:128], vaug[64 * c:64 * c + 64, :], start=True, stop=True)
                    return e.matmul(p7[:, 0:257], kw[64 * c:64 * c + 64, 128:256], vaug[64 * c:64 * c + 64, :],
                                    start=True, stop=True)
                PE(dmm, [b_kw, b_v], [b_p6, b_p7])

                def cev(e):
                    e.activation(out=dtmp[:, 0, :], in_=p6[:, 0:257], func=AF.Identity)
                    return e.activation(out=dtmp[:, 1, :], in_=p7[:, 0:257], func=AF.Identity)
                ACT(cev, [b_p6, b_p7], [b_dt])

                def cup(e):
                    e.scalar_tensor_tensor(out=Cf[dst][:, 0, :], in0=Cf[src][:, 0, :], scalar=sc[:, 12 + c:13 + c],
                                           in1=dtmp[:, 0, :], op0=ALU.mult, op1=ALU.add)
                    return e.scalar_tensor_tensor(out=Cf[dst][:, 1, :], in0=Cf[src][:, 1, :], scalar=sc[:, 12 + c:13 + c],
                                                  in1=dtmp[:, 1, :], op0=ALU.mult, op1=ALU.add)
                DVE(cup, [b_dt, b_Cf[src], bs("exps")], [b_Cf[dst]])
                ACT(lambda e: e.activation(out=Cb[dst][:], in_=Cf[dst][:], func=AF.Identity), [b_Cf[dst]], [b_Cb[dst]])
            if nt == DEBUG.get("dump_nt", -1):
                dump("QTa", QTa[:].rearrange("p a b -> p (a b)"), 256, [b_qt])
                dump("QTb", QTb[:].rearrange("p a b -> p (a b)"), 256, [b_qt])
                dump("KTm", KTm[:].rearrange("p a b -> p (a b)"), 256, [b_kt])
                dump("sTm", sTm[:], 128, [b_sT])
                dump("kw", kw[:], 256, [b_kw])
                dump("C0", Cf[0][:].rearrange("p a b -> p (a b)"), 514, [b_Cf[0]])
            upd(0, 0, 1)
            if nt == DEBUG.get("dump_nt", -1):
                dump("C1", Cf[1][:].rearrange("p a b -> p (a b)"), 514, [b_Cf[1]])
                dump("C1b", Cb[1][:].rearrange("p a b -> p (a b)"), 514, [b_Cb[1]])

            def nmm(e):
                ins = None
                e.matmul(p5[:, 0:257], sTm[:], vaug[:], start=True, stop=False)
                for kc in range(2):
                    e.matmul(p5[:, 0:257], QTa[:, kc, :], Cb[0][:, kc, :], start=False, stop=False)
                    ins = e.matmul(p5[:, 0:257], QTb[:, kc, :], Cb[1][:, kc, :], start=False, stop=(kc == 1))
                return ins
            PE(nmm, [b_sT, b_v, b_qt, b_Cb[0], b_Cb[1]], [b_p5])
            if nt == DEBUG.get("dump_nt", -1):
                dump("num", p5[:, 0:257], 257, [b_p5])
            upd(1, 1, 0)
            if DEBUG.get('stopA', 99) <= 4:
                continue
            ACT(lambda e: e.activation(out=sc[:, 16:17], in_=p5[:, 256:257], func=AF.Abs, scale=sc[:, 10:11]),
                [b_p5, bs("exps")], [bs("dd")])
            DVE(lambda e: e.tensor_scalar_max(out=sc[:, 17:18], in0=sc[:, 16:17], scalar1=1.0), [bs("dd")], [bs("ad")])
            DVE(lambda e: e.reciprocal(out=sc[:, 17:18], in_=sc[:, 17:18]), [bs("ad")], [bs("ad")])
            DVE(lambda e: e.tensor_tensor(out=sc[:, 18:19], in0=sc[:, 10:11], in1=sc[:, 17:18], op=ALU.mult),
                [bs("ad"), bs("exps")], [bs("A")])
            DVE(lambda e: e.tensor_tensor(out=sc[:, 20:21], in0=sc[:, 18:19], in1=sc[:, 18:19], op=ALU.mult),
                [bs("A")], [bs("a2")])
            ACT(lambda e: e.activation(out=junk[:], in_=p5[:, 0:256], func=AF.Square, scale=1.0 / 16.0), [b_p5], [b_junk])
            DVE(lambda e: e.reduce_sum(out=sc[:, 19:20], in_=junk[:], axis=AX), [b_junk], [bs("ssq")])
            ACT(lambda e: e.activation(out=sc[:, 21:22], in_=sc[:, 19:20], func=AF.Ln, bias=EPS, scale=sc[:, 20:21]),
                [bs("ssq"), bs("a2")], [bs("sr")])
            ACT(lambda e: e.activation(out=sc[:, 21:22], in_=sc[:, 21:22], func=AF.Exp, scale=-0.5), [bs("sr")], [bs("sr")])
            DVE(lambda e: e.tensor_tensor(out=sc[:, 22:23], in0=sc[:, 21:22], in1=sc[:, 18:19], op=ALU.mult),
                [bs("sr"), bs("A")], [bs("scl")])
            ACT(lambda e: e.activation(out=junk[:], in_=p5[:, 0:256], func=AF.Identity, scale=sc[:, 22:23]),
                [b_p5, bs("scl")], [b_junk])
            DVE(lambda e: e.tensor_tensor(out=hatm[:], in0=junk[:], in1=sgo[:], op=ALU.mult), [b_junk, b_go], [b_ha])
            if nt == DEBUG.get("dump_nt", -1):
                dump("sc2", sc[:, 16:24], 8, [bs("scl"), bs("sr"), bs("A"), bs("a2"), bs("ssq"), bs("ad"), bs("dd")])
                dump("hatm", hatm[:], 256, [b_ha])

            if DEBUG.get('stopA', 99) <= 5:
                continue
            def otr(e):
                e.transpose(tpsb[:, 512:640], hatm[:, 0:128], identb[:])
                return e.transpose(tpsb[:, 640:768], hatm[:, 128:256], identb[:])
            PE(otr, [b_ha, b_id], [b_tps])
            j = nt % 4

            def oev(e, j=j):
                e.activation(out=ostA[:, 0, j * 128:(j + 1) * 128], in_=tpsb[:, 512:640], func=AF.Identity)
                return e.activation(out=ostA[:, 1, j * 128:(j + 1) * 128], in_=tpsb[:, 640:768], func=AF.Identity)
            ACT(oev, [b_tps], [b_oA])
            if j == 3:
                t0 = (nt - 3) * 128
                P.op("sp", lambda e, t0=t0: e.dma_start(out=mixo[0:256, t0:t0 + 512].rearrange("(c p) t -> p c t", p=128),
                                                        in_=ostA[:]), reads=[b_oA], writes=[b_out], chan=c_oA)

        b_qk, b_F, b_oB, b_rec, b_acc = [B() for _ in range(5)]
        b_QT = [B(), B()]
        b_crow = [B(), B()]
        b_b4 = [B(), B()]
        b_KT = [B() for _ in range(NT)]
        b_V = [B() for _ in range(NT)]
        b_st = [b_p4, b_p5]
        b_PT = [B(), B(), B()]
        stt = [p4, p5]

        def tile_gen(kb, hh, firstB):
            NBX = 18 + hh
            r = rtab[:, 0, kb:kb + 1]
            r2 = rtab[:, 2, kb:kb + 1]
            j = kb % 4
            par = (kb // 4) % 2
            cR0 = 33 + par
            cdR = 36 + 4 * par + j
            ACT(lambda e: e.activation(out=junk[:], in_=z0[:, 0:256], func=AF.Square, scale=128.0 ** -0.5), [b_z], [b_junk])
            yield
            DVE(lambda e: e.reduce_sum(out=sc[:, 24:26], in_=junk[:].rearrange("p (a b) -> p a b", a=2), axis=AX),
                [b_junk], [bs("ssq2")])
            yield
            ACT(lambda e: e.activation(out=sc[:, 26:28], in_=sc[:, 24:26], func=AF.Ln, bias=EPS, scale=r2),
                [bs("ssq2"), b_rt2], [bs("s2")])
            yield
            ACT(lambda e: e.activation(out=sc[:, 26:28], in_=sc[:, 26:28], func=AF.Exp, scale=-0.5), [bs("s2")], [bs("s2")])
            yield
            DVE(lambda e: e.tensor_scalar(out=sc[:, 28:30], in0=sc[:, 26:28], scalar1=r, scalar2=None, op0=ALU.mult),
                [bs("s2"), b_rt], [bs("sc2")])
            yield

            def s2b(e):
                e.activation(out=junk[:, 0:128], in_=z0[:, 0:128], func=AF.Identity, scale=sc[:, 28:29])
                return e.activation(out=junk[:, 128:256], in_=z0[:, 128:256], func=AF.Identity, scale=sc[:, 29:30])
            ACT(s2b, [bs("sc2"), b_z], [b_junk])
            ACT(lambda e: e.activation(out=V[:, kb, :], in_=z0[:, 256:384], func=AF.Identity, scale=r), [b_z, b_rt], [b_V[kb]])
            ACT(lambda e: e.activation(out=sc[:, 30:31], in_=z0[:, 384:385], func=AF.Identity, scale=r,
                                       bias=vec[:, NBX:NBX + 1]), [b_z, b_rt, b_vec], [bs("yB")])
            yield
            if kb + 1 < NT:
                front(kb + 1, [(z0, 0, 385)], firstB)
            yield
            DVE(lambda e: e.tensor_tensor(out=qkh[:], in0=junk[:], in1=gqk[:], op=ALU.mult), [b_junk, b_cst], [b_qk])
            ACT(lambda e: e.activation(out=sc[:, 30:31], in_=sc[:, 30:31], func=AF.Exp, scale=-1.0), [bs("yB")], [bs("yB")])
            yield
            ACT(lambda e: e.activation(out=sc[:, 31:32], in_=sc[:, 30:31], func=AF.Ln, bias=1.0, scale=1.0), [bs("yB")], [bs("l1pB")])

            def tr2(e):
                e.transpose(tpsb[:, 0:128], qkh[:, 0:128], identb[:])
                return e.transpose(tpsb[:, 128:256], qkh[:, 128:256], identb[:])
            PE(tr2, [b_qk, b_id], [b_tps])
            yield
            ACT(lambda e: e.activation(out=QT2[:, par, j * 128:(j + 1) * 128], in_=tpsb[:, 0:128], func=AF.Identity),
                [b_tps], [b_QT[par]])
            ACT(lambda e: e.activation(out=KT[:, kb * 128:(kb + 1) * 128], in_=tpsb[:, 128:256], func=AF.Identity),
                [b_tps], [b_KT[kb]])

            def fm1(e):
                e.matmul(z2[:, 40:41], C(5), sc[:, 31:32], start=True, stop=True)
                return e.matmul(z2[:, 41:42], C(6), sc[:, 31:32], start=True, stop=True)
            PE(fm1, [bs("l1pB"), b_cst], [b_zf])
            yield
            DVE(lambda e: e.tensor_tensor(out=Fg[:, kb:kb + 1], in0=Fblk[:], in1=z2[:, 40:41], op=ALU.add), [b_zf, b_Fblk], [b_F])
            DVE(lambda e: e.tensor_tensor(out=Fblk[:], in0=Fblk[:], in1=z2[:, 41:42], op=ALU.add), [b_zf, b_Fblk], [b_Fblk])
            yield
            PE(lambda e: e.matmul(z2[:, 48:49], C(7), Fg[:, kb:kb + 1], start=True, stop=True), [b_F, b_cst], [b_zr])
            yield
            DVE(lambda e: e.tensor_copy(out=sc[:, 32:33], in_=z2[:, 48:49]), [b_zr], [bs("Rb")])
            yield
            if j == 0:
                DVE(lambda e: e.tensor_copy(out=sc[:, cR0:cR0 + 1], in_=sc[:, 32:33]), [bs("Rb")], [bs("Rb0_%d" % par)])
                yield
            DVE(lambda e: e.tensor_tensor(out=sc[:, cdR:cdR + 1], in0=sc[:, 32:33], in1=sc[:, cR0:cR0 + 1], op=ALU.subtract),
                [bs("Rb"), bs("Rb0_%d" % par)], [bs("dR%d" % cdR)])
            yield
            ACT(lambda e: e.activation(out=crow2[0:1, par, j * 128:(j + 1) * 128], in_=zrow[0:1, :], func=AF.Identity,
                                       bias=sc[0:1, cdR:cdR + 1], scale=1.0), [bs("dR%d" % cdR), b_id], [b_crow[par]])
            yield
            if j == 3:
                nkb = kb + 1
                DVE(lambda e: e.tensor_scalar(out=biasq2[:, par, 0:nkb], in0=Fg[:, 0:nkb], scalar1=sc[:, cR0:cR0 + 1],
                                              scalar2=cst[:, 6 * 128:6 * 128 + 1], op0=ALU.subtract, op1=ALU.mult),
                    [b_F, bs("Rb0_%d" % par), b_cst], [b_b4[par]])
                yield

        def attn_gen(Q, hh):
            par = Q % 2
            nkb = 4 * Q + 4

            def qk(kbp):
                c0 = 128 * max(0, kbp - 4 * Q)
                si = kbp % 2

                def qkmm(e):
                    e.matmul(stt[si][:, c0:512], KT[:, kbp * 128:(kbp + 1) * 128], QT2[:, par, c0:512], start=True, stop=False)
                    return e.matmul(stt[si][:, c0:512], e0b[:], crow2[:, par, c0:512], start=False, stop=True)
                PE(qkmm, [b_KT[kbp], b_QT[par], b_crow[par], b_id], [b_st[si]])
            qk(0)
            for kbp in range(nkb):
                c0 = 128 * max(0, kbp - 4 * Q)
                si = kbp % 2
                pi = kbp % 3
                ACT(lambda e, c0=c0, si=si, pi=pi, kbp=kbp: e.activation(
                    out=PT[pi][:, c0:512], in_=stt[si][:, c0:512], func=AF.Exp, bias=biasq2[:, par, kbp:kbp + 1], scale=1.0),
                    [b_st[si], b_b4[par]], [b_PT[pi]])
                if kbp >= 4 * Q:
                    DVE(lambda e, pi=pi, c0=c0: e.tensor_tensor(out=PT[pi][:, c0:c0 + 128], in0=PT[pi][:, c0:c0 + 128],
                                                                in1=C(9), op=ALU.mult), [b_PT[pi], b_cst], [b_PT[pi]])
                if kbp + 1 < nkb:
                    qk(kbp + 1)
                PE(lambda e, pi=pi, c0=c0, kbp=kbp: e.matmul(
                    p6[:, c0:512], V[:, kbp, :], PT[pi][:, c0:512], start=(kbp == 0), stop=(kbp == nkb - 1)),
                    [b_V[kbp], b_PT[pi]], [b_p6])
                if kbp == 0:
                    DVE(lambda e, pi=pi: e.tensor_copy(out=acc[:], in_=PT[pi][:]), [b_PT[pi]], [b_acc])
                else:
                    DVE(lambda e, pi=pi, c0=c0: e.tensor_tensor(out=acc[:, c0:512], in0=acc[:, c0:512], in1=PT[pi][:, c0:512],
                                                                op=ALU.add), [b_PT[pi], b_acc], [b_acc])
                yield
            PE(lambda e: e.matmul(p7[:], C(6), acc[:], start=True, stop=True), [b_acc, b_cst], [b_p7])
            ACT(lambda e: e.activation(out=rec[:], in_=p7[:], func=AF.Identity, scale=-1.0), [b_p7], [b_rec])
            DVE(lambda e: e.reciprocal(out=rec[:], in_=rec[:]), [b_rec], [b_rec])
            DVE(lambda e: e.tensor_tensor(out=ostB[:], in0=rec[:], in1=p6[:], op=ALU.mult), [b_p6, b_rec], [b_oB])
            r0 = 256 + hh * 128
            P.op("sp", lambda e: e.dma_start(out=mixo[r0:r0 + 128, Q * 512:(Q + 1) * 512], in_=ostB[:]),
                 reads=[b_oB], writes=[b_out], chan=c_oB)
            yield

        def fe_gen(Q, hh, firstB):
            for kb in range(4 * Q, 4 * Q + 4):
                yield from tile_gen(kb, hh, firstB)

        for hh in range(2 if not DEBUG.get("skipB") else 0):
            load_weights(wBin[hh], 385)
            P.op("pool", lambda e: e.memset(Fblk[:], 0.0), writes=[b_Fblk])
            firstB = bool(DEBUG.get("skipA")) and hh == 0
            front(0, [(z0, 0, 385)], firstB)
            pending = None
            for Q in range(NT // 4):
                fe = fe_gen(Q, hh, firstB)
                if pending is not None and not DEBUG.get("nointerleave"):
                    nsteps = 4 * Q
                    per = max(1, -(-FE_YIELDS // max(1, nsteps)))
                    for _ in pending:
                        for _ in range(per):
                            next(fe, None)
                elif pending is not None:
                    for _ in pending:
                        pass
                for _ in fe:
                    pass
                pending = attn_gen(Q, hh)
            for _ in pending:
                pass
        P.op("sp", lambda e: None, reads=[b_out])
        P.run()
    return nc


def phase1_inputs(inp, S):
    w_in = inp["w_in"][0]
    x = inp["x"]
    NT = S // 128
    consts = phase1_consts()
    maps = []
    xTs = []
    for b in range(2):
        xTs.append(np.ascontiguousarray(x[b].reshape(NT, 128, NFC, 128).transpose(0, 3, 2, 1).reshape(NT, 128, D)))
    gb = inp["mlstm_gate_bias"][0]
    fb = inp["fox_f_bias"][0]
    ar = np.arange
    for c in range(8):
        b, g = c // 4, c % 4
        colsA = np.concatenate([g * 256 + ar(256), 1024 + g * 256 + ar(256), 2048 + g * 256 + ar(256),
                                3072 + g * 256 + ar(256), [4096 + g], [4100 + g]])
        wA = np.ascontiguousarray(w_in[:, colsA].reshape(NFC, 128, 1026))
        wB = []
        for hh in range(2):
            h = 2 * g + hh
            colsB = np.concatenate([4104 + h * 128 + ar(128), 5128 + h * 128 + ar(128), 6152 + h * 128 + ar(128),
                                    [7176 + h]])
            wB.append(w_in[:, colsB].reshape(NFC, 128, 385))
        wB = np.ascontiguousarray(np.stack(wB, 0))
        v = np.zeros((128, 20), np.float32)
        v[:, 0:16] = inp["mix_norm_g"][0].reshape(16, 128).T
        v[:, 16] = gb[g]
        v[:, 17] = gb[4 + g]
        v[:, 18] = fb[2 * g]
        v[:, 19] = fb[2 * g + 1]
        gout = np.ascontiguousarray(np.broadcast_to(inp["mlstm_out_g"][0][g * 256:(g + 1) * 256], (128, 256)))
        gqk = np.ascontiguousarray(np.broadcast_to(np.concatenate([inp["fox_q_g"][0], inp["fox_k_g"][0]]), (128, 256)))
        maps.append({"xT": xTs[b], "wA": wA, "wB": wB, "vecs_in": v, "gout_in": gout, "gqk_in": gqk, "consts": consts})
    return maps


def run_phase1(inp, S):
    nc = build_phase1(S)
    res = run_bass_kernel_spmd(nc, phase1_inputs(inp, S), core_ids=list(range(8)))
    mix = np.zeros((2, D, S), NPBF)
    for c in range(8):
        b, g = c // 4, c % 4
        mix[b, g * 512:(g + 1) * 512, :] = np.asarray(res.results[c]["mixo"])
    if DEBUG.get("dump"):
        DEBUG["dbg"] = [np.asarray(res.results[c]["dbg"]) for c in range(8)]
    return mix


def kernel(**inp):
    inp = {k: np.asarray(v) for k, v in inp.items()}
    S = inp["x"].shape[1]
    mix = run_phase1(inp, S)
    return run_phase2(inp, mix, S)
```
